# Optimizing a Trainium2 kernel written in Bass

```python
import math
import jax, jax.numpy as jnp
from jax import lax
import numpy as np

D_MODEL = 2048
BATCH = 2
SEQ = 4096
DEPTH = 1
DEC_BATCH = 32
DEC_SEQ = 8
PAST_LEN = 8192
PAGE_SIZE = 128

N_HEADS = 16
N_KV_HEADS = 4
HEAD_DIM = 128
KV_GROUP = N_HEADS // N_KV_HEADS
ATTN_WIDTH = N_HEADS * HEAD_DIM
KV_WIDTH = N_KV_HEADS * HEAD_DIM
ROPE_THETA = 10000.0
MOBA_BLOCK = 256
MOBA_TOPK = 3
MOBA_NSEL = MOBA_TOPK + 1
QUERY_ROWS = 128
D_RNN = D_MODEL
RNN_BLOCKS = 16
RNN_BLOCK_W = D_RNN // RNN_BLOCKS
CONV_W = 4
LRU_C = 8.0
D_FF = 4 * D_MODEL
RMS_EPS = 1e-6
NEG_INF = -1e30
IN_WIDTH = ATTN_WIDTH + 2 * KV_WIDTH + 2 * D_RNN + 2 * D_MODEL

kernel_name = 'griffin_moba_hybrid_step'


def rms_norm(x, g):
    xf = x.astype(jnp.float32)
    y = xf * lax.rsqrt(jnp.mean(xf * xf, axis=-1, keepdims=True) + RMS_EPS)
    return (y * g.astype(jnp.float32)).astype(x.dtype)


def rope(x, pos):
    half = HEAD_DIM // 2
    inv = ROPE_THETA ** (-jnp.arange(half, dtype=jnp.float32) * (2.0 / HEAD_DIM))
    ang = pos.astype(jnp.float32)[:, None] * inv[None, :]
    cos = jnp.cos(ang)[None, :, None, :]
    sin = jnp.sin(ang)[None, :, None, :]
    xf = x.astype(jnp.float32)
    x1, x2 = xf[..., :half], xf[..., half:]
    return jnp.concatenate([x1 * cos - x2 * sin, x2 * cos + x1 * sin], axis=-1).astype(x.dtype)


def _lin_combine(left, right):
    a1, b1 = left
    a2, b2 = right
    return a1 * a2, a2 * b1 + b2


def rg_lru_branch(xr, yg, conv_buf, h0, conv_w, conv_b, w_rg_a, b_rg_a, w_rg_x, b_rg_x, lru_lambda):
    B, S, C = xr.shape
    xp = jnp.concatenate([conv_buf.astype(xr.dtype), xr], axis=1)
    xc = conv_b
    for j in range(CONV_W):
        xc = xc + xp[:, j:j + S] * conv_w[j]
    new_buf = xp[:, S:]
    xb = xc.reshape(B, S, RNN_BLOCKS, RNN_BLOCK_W)
    r = jax.nn.sigmoid((jnp.einsum('bsnc,ncd->bsnd', xb, w_rg_a).reshape(B, S, C) + b_rg_a).astype(jnp.float32))
    i = jax.nn.sigmoid((jnp.einsum('bsnc,ncd->bsnd', xb, w_rg_x).reshape(B, S, C) + b_rg_x).astype(jnp.float32))
    log_a = -LRU_C * r * jax.nn.softplus(-lru_lambda.astype(jnp.float32))
    a = jnp.exp(log_a)
    b = jnp.sqrt(-jnp.expm1(2.0 * log_a)) * (i * xc.astype(jnp.float32))
    a_cum, h = lax.associative_scan(_lin_combine, (a, b), axis=1)
    h = h + a_cum * h0.astype(jnp.float32)[:, None, :]
    out = h.astype(xr.dtype) * jax.nn.gelu(yg)
    return out, h[:, -1].astype(h0.dtype), new_buf


def moba_attention(q, k, v, q_pos):
    B, Sq = q.shape[0], q.shape[1]
    Sk = k.shape[1]
    nb = -(-Sk // MOBA_BLOCK)
    pad = nb * MOBA_BLOCK - Sk
    kb = jnp.pad(k, ((0, 0), (0, pad), (0, 0), (0, 0))).reshape(B, nb, MOBA_BLOCK, N_KV_HEADS, HEAD_DIM)
    vb = jnp.pad(v, ((0, 0), (0, pad), (0, 0), (0, 0))).reshape(B, nb, MOBA_BLOCK, N_KV_HEADS, HEAD_DIM)
    k_mean = jnp.repeat(jnp.mean(kb.astype(jnp.float32), axis=2), KV_GROUP, axis=2)
    cap = max(1, QUERY_ROWS // B)
    qc = max(d for d in range(1, min(cap, Sq) + 1) if Sq % d == 0)
    n_chunks = Sq // qc
    q_chunks = q.reshape(B, n_chunks, qc, N_HEADS, HEAD_DIM).transpose(1, 0, 3, 2, 4)
    pos_chunks = q_pos.reshape(n_chunks, qc)
    b_idx = jnp.arange(B)[:, None, None, None]
    kv_idx = (jnp.arange(N_HEADS) // KV_GROUP)[None, :, None, None]
    blk_ids = jnp.arange(max(nb, MOBA_TOPK))
    scale = HEAD_DIM ** -0.5

    def one_chunk(args):
        qi, pi = args
        own = pi // MOBA_BLOCK
        gate = jnp.einsum('bhqd,bnhd->bhqn', qi.astype(jnp.float32), k_mean)
        if nb < MOBA_TOPK:
            gate = jnp.pad(gate, ((0, 0), (0, 0), (0, 0), (0, MOBA_TOPK - nb)))
        gate = jnp.where(blk_ids < own[:, None], gate, NEG_INF)
        _, top = lax.top_k(gate, MOBA_TOPK)
        valid = jnp.arange(MOBA_TOPK) < own[:, None]
        sel = jnp.where(valid, top, own[:, None])
        sel = jnp.concatenate([sel, jnp.broadcast_to(own[:, None], (B, N_HEADS, qc, 1))], axis=-1)
        valid = jnp.concatenate([valid, jnp.ones((qc, 1), dtype=bool)], axis=-1)
        kg = kb[b_idx, sel, :, kv_idx]
        vg = vb[b_idx, sel, :, kv_idx]
        key_pos = sel[..., None] * MOBA_BLOCK + jnp.arange(MOBA_BLOCK)
        ok = valid[:, :, None] & (key_pos <= pi[:, None, None])
        s = jnp.einsum('bhqd,bhqnkd->bhqnk', qi, kg).astype(jnp.float32) * scale
        s = jnp.where(ok, s, NEG_INF).reshape(B, N_HEADS, qc, MOBA_NSEL * MOBA_BLOCK)
        p = jax.nn.softmax(s, axis=-1).reshape(B, N_HEADS, qc, MOBA_NSEL, MOBA_BLOCK)
        return jnp.einsum('bhqnk,bhqnkd->bhqd', p.astype(vg.dtype), vg)

    out = lax.map(one_chunk, (q_chunks, pos_chunks))
    return out.transpose(1, 0, 3, 2, 4).reshape(B, Sq, ATTN_WIDTH)


def decoder_layer(x, pos, k_past, v_past, h0, conv_buf, norm_mix, w_in, b_gate, conv_w, conv_b,
                  w_rg_a, b_rg_a, w_rg_x, b_rg_x, lru_lambda, w_proj_a, w_proj_b, w_out,
                  norm_mlp, w_ff1, w_ff2):
    B, S, _ = x.shape
    proj = rms_norm(x, norm_mix) @ w_in
    o1 = ATTN_WIDTH
    o2 = o1 + KV_WIDTH
    o3 = o2 + KV_WIDTH
    o4 = o3 + D_RNN
    o5 = o4 + D_RNN
    q, k, v, xr, yg, gates = jnp.split(proj, [o1, o2, o3, o4, o5], axis=-1)
    q = rope(q.reshape(B, S, N_HEADS, HEAD_DIM), pos)
    k = rope(k.reshape(B, S, N_KV_HEADS, HEAD_DIM), pos)
    v = v.reshape(B, S, N_KV_HEADS, HEAD_DIM)
    if k_past is None:
        k_all, v_all = k, v
    else:
        k_all = jnp.concatenate([k_past.astype(k.dtype), k], axis=1)
        v_all = jnp.concatenate([v_past.astype(v.dtype), v], axis=1)
    attn = moba_attention(q, k_all, v_all, pos)
    rnn, h_last, conv_new = rg_lru_branch(xr, yg, conv_buf, h0, conv_w, conv_b,
                                          w_rg_a, b_rg_a, w_rg_x, b_rg_x, lru_lambda)
    g = jax.nn.sigmoid((gates + b_gate).astype(jnp.float32)).astype(x.dtype)
    merged = g[..., :D_MODEL] * (rnn @ w_proj_a) + g[..., D_MODEL:] * (attn @ w_proj_b)
    x = x + merged @ w_out
    x = x + jnp.square(jax.nn.relu(rms_norm(x, norm_mlp) @ w_ff1)) @ w_ff2
    return x, k, v, h_last, conv_new


def setup_inputs(seed: int = 0) -> dict:
    key = jax.random.key(seed)
    ks = jax.random.split(key, 26)
    n_pages = PAST_LEN // PAGE_SIZE
    n_used = DEC_BATCH * n_pages
    n_phys = n_used + -(-n_used // 4)

    def nrm(k, shape, scale):
        return jax.random.normal(k, shape, jnp.float32) * scale

    page_table = jax.random.permutation(ks[6], n_phys)[:n_used].reshape(DEC_BATCH, n_pages).astype(jnp.int32)
    u = jax.random.uniform(ks[16], (DEPTH, D_RNN), jnp.float32, minval=0.9, maxval=0.999)
    a_base = u ** (1.0 / LRU_C)
    lru_lambda = jnp.log(a_base) - jnp.log1p(-a_base)
    return {
        'x_prompt': nrm(ks[0], (BATCH, SEQ, D_MODEL), 1.0),
        'x_sample': nrm(ks[1], (DEC_BATCH, DEC_SEQ, D_MODEL), 1.0),
        'cache_k': nrm(ks[2], (DEPTH, n_phys, PAGE_SIZE, N_KV_HEADS, HEAD_DIM), 1.0),
        'cache_v': nrm(ks[3], (DEPTH, n_phys, PAGE_SIZE, N_KV_HEADS, HEAD_DIM), 1.0),
        'state_h': nrm(ks[4], (DEPTH, DEC_BATCH, D_RNN), 0.5),
        'state_conv': nrm(ks[5], (DEPTH, DEC_BATCH, CONV_W - 1, D_RNN), 1.0),
        'page_table': page_table,
        'norm_mix': 1.0 + nrm(ks[7], (DEPTH, D_MODEL), 0.02),
        'w_in': nrm(ks[8], (DEPTH, D_MODEL, IN_WIDTH), D_MODEL ** -0.5),
        'b_gate': nrm(ks[9], (DEPTH, 2 * D_MODEL), 0.1),
        'conv_w': nrm(ks[10], (DEPTH, CONV_W, D_RNN), CONV_W ** -0.5),
        'conv_b': nrm(ks[11], (DEPTH, D_RNN), 0.02),
        'w_rg_a': nrm(ks[12], (DEPTH, RNN_BLOCKS, RNN_BLOCK_W, RNN_BLOCK_W), RNN_BLOCK_W ** -0.5),
        'b_rg_a': nrm(ks[13], (DEPTH, D_RNN), 0.1),
        'w_rg_x': nrm(ks[14], (DEPTH, RNN_BLOCKS, RNN_BLOCK_W, RNN_BLOCK_W), RNN_BLOCK_W ** -0.5),
        'b_rg_x': nrm(ks[15], (DEPTH, D_RNN), 0.1),
        'lru_lambda': lru_lambda,
        'w_proj_a': nrm(ks[17], (DEPTH, D_RNN, D_MODEL), D_RNN ** -0.5),
        'w_proj_b': nrm(ks[18], (DEPTH, ATTN_WIDTH, D_MODEL), ATTN_WIDTH ** -0.5),
        'w_out': nrm(ks[19], (DEPTH, D_MODEL, D_MODEL), D_MODEL ** -0.5),
        'norm_mlp': 1.0 + nrm(ks[20], (DEPTH, D_MODEL), 0.02),
        'w_ff1': nrm(ks[21], (DEPTH, D_MODEL, D_FF), D_MODEL ** -0.5),
        'w_ff2': nrm(ks[22], (DEPTH, D_FF, D_MODEL), D_FF ** -0.5),
        'norm_final': 1.0 + nrm(ks[23], (D_MODEL,), 0.02),
    }


def reference(x_prompt, x_sample, cache_k, cache_v, state_h, state_conv, page_table,
              norm_mix, w_in, b_gate, conv_w, conv_b, w_rg_a, b_rg_a, w_rg_x, b_rg_x,
              lru_lambda, w_proj_a, w_proj_b, w_out, norm_mlp, w_ff1, w_ff2, norm_final):
    B, S, _ = x_prompt.shape
    DB, DS, _ = x_sample.shape
    past_len = page_table.shape[1] * cache_k.shape[2]
    pos_p = jnp.arange(S, dtype=jnp.int32)
    pos_s = past_len + jnp.arange(DS, dtype=jnp.int32)
    hp, hs = x_prompt, x_sample
    kp, vp, rp, cp, ks_, vs_, rs, cs = [], [], [], [], [], [], [], []
    for l in range(DEPTH):
        lw = (norm_mix[l], w_in[l], b_gate[l], conv_w[l], conv_b[l], w_rg_a[l], b_rg_a[l],
              w_rg_x[l], b_rg_x[l], lru_lambda[l], w_proj_a[l], w_proj_b[l], w_out[l],
              norm_mlp[l], w_ff1[l], w_ff2[l])
        h0_p = jnp.zeros((B, D_RNN), x_prompt.dtype)
        buf_p = jnp.zeros((B, CONV_W - 1, D_RNN), x_prompt.dtype)
        hp, k1, v1, r1, c1 = decoder_layer(hp, pos_p, None, None, h0_p, buf_p, *lw)
        k_past = cache_k[l][page_table].reshape(DB, past_len, N_KV_HEADS, HEAD_DIM)
        v_past = cache_v[l][page_table].reshape(DB, past_len, N_KV_HEADS, HEAD_DIM)
        hs, k2, v2, r2, c2 = decoder_layer(hs, pos_s, k_past, v_past, state_h[l], state_conv[l], *lw)
        kp.append(k1); vp.append(v1); rp.append(r1); cp.append(c1)
        ks_.append(k2); vs_.append(v2); rs.append(r2); cs.append(c2)
    y_prompt = rms_norm(hp, norm_final)
    y_sample = rms_norm(hs, norm_final)
    return (y_prompt, y_sample, jnp.stack(kp), jnp.stack(vp), jnp.stack(rp), jnp.stack(cp),
            jnp.stack(ks_), jnp.stack(vs_), jnp.stack(rs), jnp.stack(cs))
```

```python
import contextlib
import numpy as np
import ml_dtypes
import concourse.bass as bass
import concourse.mybir as mybir
from concourse.bass_utils import run_bass_kernel_spmd

F32 = mybir.dt.float32
BF16 = mybir.dt.bfloat16
I32 = mybir.dt.int32
AF = mybir.ActivationFunctionType
ALU = mybir.AluOpType
AX = mybir.AxisListType

D = 2048
NCH = 16
SEQ = 4096
NPRE = 3072
NOWN = 1024
NSMP = 32
NT = NOWN + NSMP
TT = 352
NCOL = NPRE + NT
N_PHYS = 2560
GW = 256
NSLOT = 4
NGRP = 132
NEGM = 30000.0
SCALE = 128 ** -0.5
O_K, O_V, O_XR, O_YG, O_GA, O_GB = 2048, 2560, 3072, 5120, 7168, 9216
P_NMIX, P_NMLP, P_NFIN, P_CW0, P_CB, P_BRA, P_BRX, P_LAM, P_BGA, P_BGB = 0, 1, 2, 3, 7, 8, 9, 10, 11, 12
NPRM = 13

DEBUG = False


class _Stop(Exception):
    pass


STOP = [0]
PREF = [3]


class Buf:
    __slots__ = ("w", "r", "name", "excl")

    def __init__(self, name="", excl=False):
        self.w = {}
        self.r = {}
        self.name = name
        self.excl = excl


DT_SIZE = {F32: 4, BF16: 2, I32: 4}


class Arena:
    def __init__(self, lo, hi):
        self.free_list = [(lo, hi)]
        self.used = {}
        self.peak = 0
        self.hi = hi
        self.lo = lo

    def alloc(self, nbytes, name="", top=False):
        nbytes = (nbytes + 63) // 64 * 64
        order = range(len(self.free_list) - 1, -1, -1) if top else range(len(self.free_list))
        for i in order:
            a, b = self.free_list[i]
            if b - a >= nbytes:
                if b - a == nbytes:
                    self.free_list.pop(i)
                    off = a
                elif top:
                    self.free_list[i] = (a, b - nbytes)
                    off = b - nbytes
                else:
                    self.free_list[i] = (a + nbytes, b)
                    off = a
                self.used[off] = nbytes
                tot = sum(self.used.values())
                self.peak = max(self.peak, tot)
                return off
        raise MemoryError(f"SBUF arena exhausted allocating {name} ({nbytes}B); used={sum(self.used.values())} free={self.free_list}")

    def free(self, off):
        n = self.used.pop(off)
        self.free_list.append((off, off + n))
        self.free_list.sort()
        merged = []
        for a, b in self.free_list:
            if merged and merged[-1][1] == a:
                merged[-1] = (merged[-1][0], b)
            else:
                merged.append((a, b))
        self.free_list = merged


class Sched:
    CH = 16000

    def __init__(self, nc, es):
        self.nc, self.es = nc, es
        self.E = {"pe": nc.tensor, "act": nc.scalar, "dve": nc.vector, "pool": nc.gpsimd, "sp": nc.sync}
        self.cnt = {e: 0 for e in self.E}
        self.sems = {e: [] for e in self.E}
        self.seen = {e: {} for e in self.E}
        self.dsems = []

    def _sem(self, e, ep):
        while len(self.sems[e]) <= ep:
            self.sems[e].append(self.es.enter_context(self.nc.semaphore(f"s_{e}_{len(self.sems[e])}")))
        return self.sems[e][ep]

    def dsem(self, name):
        d = [self.es.enter_context(self.nc.semaphore("d_" + name)), 0]
        self.dsems.append(d)
        return d

    def _wait(self, e, tok):
        eng = self.E[e]
        if tok[0] == "dma":
            _, d, val = tok
            key = ("dma", id(d))
            if self.seen[e].get(key, 0) >= val:
                return
            eng.wait_ge(d[0], val)
            self.seen[e][key] = val
        else:
            p, seq = tok
            if p == e and e == "pe":
                return
            ep = (seq - 1) // self.CH
            v = (seq - 1) % self.CH + 1
            cur = self.seen[e].get(p, (-1, 0))
            if cur[0] > ep or (cur[0] == ep and cur[1] >= v):
                return
            eng.wait_ge(self._sem(p, ep), v)
            self.seen[e][p] = (ep, v)

    def _deps(self, reads, writes):
        deps = []
        for b in reads:
            deps += list(b.w.values())
            if b.excl:
                deps += list(b.r.values())
        for b in writes:
            deps += list(b.w.values())
            deps += list(b.r.values())
        return deps

    @staticmethod
    def _key(tok):
        return ("dma", id(tok[1])) if tok[0] == "dma" else tok[0]

    def _commit(self, tok, reads, writes):
        k = self._key(tok)
        for b in writes:
            b.w = {k: tok}
            b.r = {}
        for b in reads:
            if b not in writes:
                b.r[k] = tok

    def waitfor(self, e, reads=(), writes=()):
        for t in self._deps(reads, writes):
            self._wait(e, t)

    def op(self, e, fn, reads=(), writes=()):
        for t in self._deps(reads, writes):
            self._wait(e, t)
        inst = fn(self.E[e])
        self.cnt[e] += 1
        seq = self.cnt[e]
        inst.then_inc(self._sem(e, (seq - 1) // self.CH), 1)
        tok = (e, seq)
        self._commit(tok, reads, writes)
        return tok

    def dma(self, q, d, out, in_, reads=(), writes=(), **kw):
        for t in self._deps(reads, writes):
            self._wait(q, t)
        inst = self.E[q].dma_start(out=out, in_=in_, **kw)
        d[1] += 16
        inst.then_inc(d[0], 16)
        tok = ("dma", d, d[1])
        self._commit(tok, reads, writes)
        return tok

    def dmaop(self, q, d, fn, reads=(), writes=()):
        for t in self._deps(reads, writes):
            self._wait(q, t)
        inst = fn(self.E[q])
        d[1] += 16
        inst.then_inc(d[0], 16)
        tok = ("dma", d, d[1])
        self._commit(tok, reads, writes)
        return tok

    def barrier(self):
        for e in self.E:
            for p in self.E:
                if self.cnt[p] > 0:
                    self._wait(e, (p, self.cnt[p]))
            for d in self.dsems:
                if d[1] > 0:
                    self._wait(e, ("dma", d, d[1]))


def build_nc(n_phys=N_PHYS):
    nc = bass.Bass("TRN2", target_bir_lowering=False)
    es = contextlib.ExitStack()

    def din(name, shape, dt=F32):
        return nc.dram_tensor(name, list(shape), dt, kind="ExternalInput").ap()

    def dout(name, shape, dt=F32):
        return nc.dram_tensor(name, list(shape), dt, kind="ExternalOutput").ap()

    xT = din("xT", [D, NCOL])
    ropeC = din("ropeC", [128, NCOL])
    ropeS = din("ropeS", [128, NCOL])
    wall = din("wall", [NGRP * 128, NCH * GW])
    w_rga = din("w_rga", [128, 16 * 128])
    w_rgx = din("w_rgx", [128, 16 * 128])
    pp = din("pp", [128, NPRM, NCH])
    st_h = din("st_h", [128, NCH, 4])
    st_c = din("st_c", [128, NCH, 4, 3])
    flags = din("flags", [128, 4])
    pt = din("pt", [1, 256], I32)
    ckT = din("ckT", [n_phys * 512, 128])
    cv = din("cv", [n_phys * 512, 128])
    pastadd = din("pastadd", [128, 8, 16])
    pastval = din("pastval", [128, 8, 16])
    ownind = din("ownind", [128, 8, 16])
    c_E = din("c_E", [32, 32, 128], BF16)
    c_identb = din("c_identb", [128, 128], BF16)
    c_identf = din("c_identf", [128, 128])
    c_ones = din("c_ones", [128, 128], BF16)
    c_swap = din("c_swap", [128, 128], BF16)
    c_caus = din("c_caus", [128, 2, 256], BF16)
    c_causs = din("c_causs", [8, 32], BF16)

    o_yT = dout("o_yT", [D, NT])
    o_kT = dout("o_kT", [512, NT])
    o_v = dout("o_v", [NT, 512])
    o_hp = dout("o_hp", [128, NCH])
    o_cp = dout("o_cp", [128, NCH, 3])
    o_hs = dout("o_hs", [128, NCH, 4])
    o_cs = dout("o_cs", [128, NCH, 4, 3])
    dbg = {}
    if DEBUG:
        dbg["uT"] = dout("g_uT", [128, NCH, NT], BF16)
        dbg["KT"] = dout("g_KT", [128, 4, SEQ], BF16)
        dbg["V"] = dout("g_V", [128, 32, 512], BF16)
        dbg["kmean"] = dout("g_kmean", [128, 4, 16], BF16)
        dbg["attnT"] = dout("g_attnT", [128, NCH, NT], BF16)
        dbg["rnnT"] = dout("g_rnnT", [128, NCH, NT], BF16)
        dbg["mergedT"] = dout("g_mergedT", [128, NCH, NT], BF16)
        dbg["x1"] = dout("g_x1", [128, NCH, NT])

    S = Sched(nc, es)

    def ck(k):
        if STOP[0] == k:
            S.barrier()
            raise _Stop()

    _uniq = [0]
    arena = Arena(16512, 229344)

    def sb(name, shape, dt, stack=None, top=False):
        _uniq[0] += 1
        nb = 1
        for d_ in shape[1:]:
            nb *= d_
        nb *= DT_SIZE[dt]
        off = arena.alloc(nb, name, top=top)
        t = nc.alloc_sbuf_tensor_at(f"{name}_{_uniq[0]}", list(shape), dt, offset=off)
        (stack or es).callback(arena.free, off)
        return t

    with es:
        try:
            E_sb = sb("E_sb", [32, 32, 128], BF16)
            identb = sb("identb", [128, 128], BF16)
            identf = sb("identf", [128, 128], F32)
            ones = sb("ones", [128, 128], BF16)
            swapm = sb("swapm", [128, 128], BF16)
            caus = sb("caus", [128, 2, 256], BF16)
            causs = sb("causs", [8, 32], BF16)
            prm = sb("prm", [128, NPRM, NCH], F32)
            cvec = sb("cvec", [128, NCH], F32)
            flg = sb("flg", [128, 4], F32)
            hcar = sb("hcar", [128, NCH], F32)
            halo = sb("halo", [128, NCH, 3], F32)
            sth = sb("sth", [128, NCH, 4], F32)
            stc = sb("stc", [128, NCH, 4, 3], F32)
            wsl = [sb(f"wsl{i}", [128, NCH, GW], BF16) for i in range(NSLOT)]
            B_const = Buf("const")
            B_w = [Buf(f"w{i}") for i in range(NSLOT)]
            d_w = [S.dsem(f"w{i}") for i in range(NSLOT)]
            d_misc = S.dsem("misc")
            d_wrg = S.dsem("wrg")
            d_out = S.dsem("out")
            d_rop = S.dsem("rop")
            d_msk = S.dsem("msk")
            d_ko = [S.dsem(f"ko{i}") for i in range(2)]
            d_vo = [S.dsem(f"vo{i}") for i in range(2)]
            d_yo = [S.dsem(f"yo{i}") for i in range(2)]
            B_car = Buf("car")
            B_halo = Buf("halo")

            ps_t = [es.enter_context(nc.psum_tensor(f"ps{i}", [128, 512], F32)) for i in range(8)]
            ps_b = [Buf(f"ps{i}", excl=True) for i in range(8)]
            ps_free = list(range(8))

            def ps_get():
                i = ps_free.pop(0)
                return i

            def ps_put(i):
                ps_free.append(i)

            for dst, src in [(E_sb, c_E), (identb, c_identb), (identf, c_identf), (ones, c_ones), (swapm, c_swap),
                             (caus, c_caus), (causs, c_causs), (prm, pp), (flg, flags), (sth, st_h), (stc, st_c)]:
                S.dma("sp", d_misc, dst[:], src, writes=[B_const])
            wrg = {}
            B_wrg = Buf("wrg")

            def load_wrg(stk):
                wrg["a"] = sb("wrga", [128, 16, 128], BF16, stk)
                wrg["x"] = sb("wrgx", [128, 16, 128], BF16, stk)
                S.dma("pool", d_wrg, wrg["a"][:].rearrange("c n d -> c (n d)"), w_rga, writes=[B_wrg])
                S.dma("pool", d_wrg, wrg["x"][:].rearrange("c n d -> c (n d)"), w_rgx, writes=[B_wrg])
            S.op("act", lambda e: e.activation(out=cvec[:], in_=prm[:, P_LAM, :], func=AF.Exp, scale=-1.0),
                 reads=[B_const], writes=[B_car])
            S.op("dve", lambda e: e.tensor_scalar(out=cvec[:], in0=cvec[:], scalar1=1.0, scalar2=None, op0=ALU.add),
                 writes=[B_car])
            S.op("act", lambda e: e.activation(out=cvec[:], in_=cvec[:], func=AF.Ln), writes=[B_car])
            S.op("dve", lambda e: e.tensor_scalar(out=cvec[:], in0=cvec[:], scalar1=-8.0, scalar2=None, op0=ALU.mult),
                 writes=[B_car])
            S.op("dve", lambda e: e.memset(hcar[:], 0.0), writes=[B_car])
            S.op("dve", lambda e: e.memset(halo[:], 0.0), writes=[B_halo])

            def wgrp(g):
                return wall[g * 128:(g + 1) * 128, :].rearrange("p (a n) -> p a n", a=2)

            w_in, w_pa, w_pb, w_o, w_f1, w_f2 = "w_in", "w_pa", "w_pb", "w_o", "w_f1", "w_f2"
            G_BASE = {"w_in": 0, "w_pa": 44, "w_pb": 52, "w_o": 60, "w_f1": 68, "w_f2": 100}

            def wview(w, r0, c0):
                return wgrp(G_BASE[w] + (r0 // D) * 8 + c0 // GW)

            plan = []
            for j in range(3):
                for k in range(2):
                    plan.append(("K", wview(w_in, 0, O_K + GW * k)))
                for k in range(2):
                    plan.append(("V", wview(w_in, 0, O_V + GW * k)))
                for k in range(8):
                    plan.append(("XR", wview(w_in, 0, O_XR + GW * k)))
            for k in range(2):
                plan.append(("K", wview(w_in, 0, O_K + GW * k)))
            for k in range(2):
                plan.append(("V", wview(w_in, 0, O_V + GW * k)))
            for k in range(8):
                plan.append(("Q", wview(w_in, 0, GW * k)))
            for k in range(8):
                plan.append(("XR", wview(w_in, 0, O_XR + GW * k)))
                plan.append(("YG", wview(w_in, 0, O_YG + GW * k)))
            for k in range(8):
                plan.append(("GA", wview(w_in, 0, O_GA + GW * k)))
                plan.append(("PA", wview(w_pa, 0, GW * k)))
                plan.append(("GB", wview(w_in, 0, O_GB + GW * k)))
                plan.append(("PB", wview(w_pb, 0, GW * k)))
            for k in range(8):
                plan.append(("WO", wview(w_o, 0, GW * k)))
            for q in range(4):
                for k in range(8):
                    plan.append(("F1", wview(w_f1, 0, D * q + GW * k)))
                for k in range(8):
                    plan.append(("F2", wview(w_f2, D * q, GW * k)))
            wst = {"issued": 0, "cons": 0}

            def w_issue(upto):
                while wst["issued"] < min(upto, len(plan)):
                    i = wst["issued"]
                    s = i % NSLOT
                    S.dma("pool", d_w[s], wsl[s][:].rearrange("p c n -> p (c n)").rearrange("p (a n) -> p a n", a=2), plan[i][1],
                          writes=[B_w[s]])
                    wst["issued"] += 1

            def w_next(tag):
                i = wst["cons"]
                assert plan[i][0] == tag, (i, plan[i][0], tag)
                w_issue(i + PREF[0])
                wst["cons"] += 1
                s = i % NSLOT
                return wsl[s], B_w[s]

            def mm_acc(psi, out_ap, pairs, reads):
                def fn(e):
                    n = len(pairs)
                    ins = None
                    for i, (l, r) in enumerate(pairs):
                        ins = e.matmul(out_ap, lhsT=l, rhs=r, start=(i == 0), stop=(i == n - 1))
                    return ins
                return S.op("pe", fn, reads=reads, writes=[ps_b[psi]])

            def tap(name, t, b):
                if DEBUG and name in dbg:
                    S.dma("sp", d_out, dbg[name], t[:], reads=[b])
                    S.barrier()

            xT_v = xT.rearrange("(c p) t -> p c t", p=128)

            def rmsnorm(stk, src_load, ncols, gidx, out_fn, tw=256, tag=""):
                xst = [sb(f"xst{tag}{i}", [128, NCH, tw], F32, stk) for i in range(2)]
                B_xst = [Buf(), Buf()]
                d_x = [S.dsem(f"x{tag}{i}") for i in range(2)]
                sqb = [sb(f"sqb{tag}{i}", [128, tw], BF16, stk) for i in range(2)]
                B_sq = [Buf(), Buf()]
                rstd = [sb(f"rstd{tag}{i}", [128, tw], F32, stk) for i in range(2)]
                B_rs = [Buf(), Buf()]
                ntile = (ncols + tw - 1) // tw
                for it in range(ntile):
                    t0 = it * tw
                    w = min(tw, ncols - t0)
                    k = it % 2
                    src_load(t0, w, xst[k], B_xst[k], d_x[k])
                    psi = ps_get()
                    for c in range(NCH):
                        kk = c % 2
                        S.op("act", lambda e, c=c, kk=kk: e.activation(out=sqb[kk][:, :w], in_=xst[k][:, c, :w], func=AF.Square),
                             reads=[B_xst[k]], writes=[B_sq[kk]])
                        S.op("pe", lambda e, c=c, kk=kk: e.matmul(ps_t[psi][:, :w], lhsT=ones[:], rhs=sqb[kk][:, :w],
                                                                     start=(c == 0), stop=(c == NCH - 1)),
                             reads=[B_sq[kk], B_const], writes=[ps_b[psi]])
                    S.op("dve", lambda e: e.tensor_scalar(out=rstd[k][:, :w], in0=ps_t[psi][:, :w], scalar1=1.0 / D, scalar2=1e-6,
                                                         op0=ALU.mult, op1=ALU.add), reads=[ps_b[psi]], writes=[B_rs[k]])
                    ps_put(psi)
                    S.op("act", lambda e: e.activation(out=rstd[k][:, :w], in_=rstd[k][:, :w], func=AF.Sqrt), writes=[B_rs[k]])
                    S.op("dve", lambda e: e.reciprocal(out=rstd[k][:, :w], in_=rstd[k][:, :w]), writes=[B_rs[k]])
                    out_fn(t0, w, xst[k], B_xst[k], rstd[k], B_rs[k])

            def norm_to_uT(stk, col0, ncols, uT, B_uT, tag, tw=256):
                def load(t0, w, t, b, d):
                    S.dma("sp", d, t[:, :, :w], xT_v[:, :, col0 + t0:col0 + t0 + w], writes=[b])

                def outf(t0, w, xs, bx, rs, br):
                    for c in range(NCH):
                        S.op("dve", lambda e, c=c: e.scalar_tensor_tensor(out=uT[:, c, t0:t0 + w], in0=xs[:, c, :w],
                                                                          scalar=prm[:, P_NMIX, c:c + 1], in1=rs[:, :w],
                                                                          op0=ALU.mult, op1=ALU.mult),
                             reads=[bx, br, B_const], writes=[B_uT])
                rmsnorm(stk, load, ncols, P_NMIX, outf, tw=tw, tag=tag)

            def rope_evac(psi, w, cosap, sinap, tmp, B_tmp, kb, B_kb, reads_tab):
                S.op("act", lambda e: e.activation(out=kb[:, :w], in_=ps_t[psi][:, :w], func=AF.Copy),
                     reads=[ps_b[psi]], writes=[B_kb])
                ck(2131)
                p2 = ps_get()
                S.op("pe", lambda e: e.matmul(ps_t[p2][:, :w], lhsT=swapm[:], rhs=kb[:, :w], start=True, stop=True),
                     reads=[B_kb, B_const], writes=[ps_b[p2]])
                ck(2132)
                S.op("dve", lambda e: e.tensor_tensor(out=tmp[0][:, :w], in0=ps_t[psi][:, :w], in1=cosap, op=ALU.mult),
                     reads=[ps_b[psi]] + reads_tab, writes=[B_tmp[0]])
                ck(2133)
                S.op("dve", lambda e: e.tensor_tensor(out=tmp[1][:, :w], in0=ps_t[p2][:, :w], in1=sinap, op=ALU.mult),
                     reads=[ps_b[p2]] + reads_tab, writes=[B_tmp[1]])
                ps_put(p2)
                ck(2134)
                S.op("dve", lambda e: e.tensor_tensor(out=tmp[0][:, :w], in0=tmp[0][:, :w], in1=tmp[1][:, :w], op=ALU.add),
                     reads=[B_tmp[1]], writes=[B_tmp[0]])

            def rglru(stk_bufs, n, xp_prompt, npr, xps, xc, N, B, h_out):
                (xcf, xcb, rr, ii, hh) = stk_bufs
                (B_xp, B_xc, B_xcb, B_r, B_i, B_h) = B
                cw = lambda j: prm[:, P_CW0 + j, n:n + 1]
                S.op("dve", lambda e: e.tensor_scalar(out=xcf[:, :npr], in0=xp_prompt[:, 0:npr], scalar1=cw(0),
                                                     scalar2=prm[:, P_CB, n:n + 1], op0=ALU.mult, op1=ALU.add),
                     reads=[B_xp, B_const], writes=[B_xc])
                for j in range(1, 4):
                    S.op("dve", lambda e, j=j: e.scalar_tensor_tensor(out=xcf[:, :npr], in0=xp_prompt[:, j:j + npr], scalar=cw(j),
                                                                      in1=xcf[:, :npr], op0=ALU.mult, op1=ALU.add),
                         reads=[B_xp], writes=[B_xc])
                if xps is not None:
                    xcs = xcf[:, npr:npr + 32].rearrange("p (s t) -> p s t", s=4)
                    S.op("dve", lambda e: e.tensor_scalar(out=xcs, in0=xps[:, :, 0:8], scalar1=cw(0),
                                                         scalar2=prm[:, P_CB, n:n + 1], op0=ALU.mult, op1=ALU.add),
                         reads=[B_xp, B_const], writes=[B_xc])
                    for j in range(1, 4):
                        S.op("dve", lambda e, j=j: e.scalar_tensor_tensor(out=xcs, in0=xps[:, :, j:j + 8], scalar=cw(j),
                                                                          in1=xcs, op0=ALU.mult, op1=ALU.add),
                             reads=[B_xp], writes=[B_xc])
                S.op("act", lambda e: e.activation(out=xcb[:, :N], in_=xcf[:, :N], func=AF.Copy), reads=[B_xc], writes=[B_xcb])
                tiles = [(t0, min(512, N - t0)) for t0 in range(0, N, 512)]
                for (wg, dst, bdst, pb) in ((wrg["a"], rr, B_r, P_BRA), (wrg["x"], ii, B_i, P_BRX)):
                    for (t0, w) in tiles:
                        psi = ps_get()
                        S.op("pe", lambda e, wg=wg, t0=t0, w=w, psi=psi: e.matmul(ps_t[psi][:, :w], lhsT=wg[:, n, :], rhs=xcb[:, t0:t0 + w],
                                                                               start=True, stop=True),
                             reads=[B_xcb, B_wrg], writes=[ps_b[psi]])
                        S.op("act", lambda e, dst=dst, t0=t0, w=w, psi=psi, pb=pb: e.activation(
                            out=dst[:, t0:t0 + w], in_=ps_t[psi][:, :w], func=AF.Sigmoid, bias=prm[:, pb, n:n + 1]),
                             reads=[ps_b[psi], B_const], writes=[bdst])
                        ps_put(psi)
                S.op("act", lambda e: e.activation(out=rr[:, :N], in_=rr[:, :N], func=AF.Exp, scale=cvec[:, n:n + 1]),
                     reads=[B_car], writes=[B_r])
                S.op("pool", lambda e: e.tensor_tensor(out=hh[:, :N], in0=rr[:, :N], in1=rr[:, :N], op=ALU.mult),
                     reads=[B_r], writes=[B_h])
                S.op("dve", lambda e: e.tensor_scalar(out=hh[:, :N], in0=hh[:, :N], scalar1=-1.0, scalar2=1.0,
                                                     op0=ALU.mult, op1=ALU.add), writes=[B_h])
                S.op("dve", lambda e: e.tensor_scalar(out=hh[:, :N], in0=hh[:, :N], scalar1=0.0, scalar2=None, op0=ALU.max),
                     writes=[B_h])
                S.op("act", lambda e: e.activation(out=hh[:, :N], in_=hh[:, :N], func=AF.Sqrt), writes=[B_h])
                S.op("pool", lambda e: e.tensor_tensor(out=ii[:, :N], in0=ii[:, :N], in1=hh[:, :N], op=ALU.mult),
                     reads=[B_h], writes=[B_i])
                S.op("pool", lambda e: e.tensor_tensor(out=ii[:, :N], in0=ii[:, :N], in1=xcf[:, :N], op=ALU.mult),
                     reads=[B_xc], writes=[B_i])
                return

            ck(1)
            stU = contextlib.ExitStack()
            uT = sb("uT", [128, NCH, NT], BF16, stU, top=True)
            stA = contextlib.ExitStack()
            KT = sb("KT", [128, 4, SEQ], BF16, stA)
            Vt = sb("Vt", [128, 32, 512], BF16, stA)
            ksum = sb("ksum", [128, 4, 16], F32, stA)
            kmean = sb("kmean", [128, 4, 16], BF16, stA)
            B_KT, B_V, B_uT, B_ksum = Buf("KT"), Buf("V"), Buf("uT"), Buf("ksum")

            for j in range(3):
                stP = contextlib.ExitStack()
                col0 = 1024 * j
                load_wrg(stP)
                ropc = sb("ropc", [128, 1024], F32, stP)
                rops = sb("rops", [128, 1024], F32, stP)
                B_rop = Buf("rop")
                S.dma("sp", d_rop, ropc[:], ropeC[:, col0:col0 + 1024], writes=[B_rop])
                S.dma("sp", d_rop, rops[:], ropeS[:, col0:col0 + 1024], writes=[B_rop])
                stN = contextlib.ExitStack()
                norm_to_uT(stN, col0, 1024, uT, B_uT, tag=f"a{j}")
                S.barrier()
                stN.close()
                if j == 0:
                    ck(2)
                tmp = [sb(f"tmpA{i}", [128, 512], F32, stP) for i in range(2)]
                B_tmp = [Buf(), Buf()]
                kb = sb("kbA", [128, 512], BF16, stP)
                B_kb = Buf()
                for k in range(2):
                    wt, bw = w_next("K")
                    if j == 0 and k == 0:
                        ck(211)
                    for gg in range(2):
                        g = 2 * k + gg
                        for tt in range(2):
                            psi = ps_get()
                            mm_acc(psi, ps_t[psi][:, :512],
                                   [(wt[:, c, gg * 128:(gg + 1) * 128], uT[:, c, tt * 512:(tt + 1) * 512]) for c in range(NCH)],
                                   [bw, B_uT])
                            if j == 0 and g == 0 and tt == 0:
                                ck(212)
                            rope_evac(psi, 512, ropc[:, tt * 512:(tt + 1) * 512], rops[:, tt * 512:(tt + 1) * 512],
                                      tmp, B_tmp, kb, B_kb, [B_rop])
                            if j == 0 and g == 0 and tt == 0:
                                ck(213)
                            ps_put(psi)
                            t0 = col0 + tt * 512
                            S.op("act", lambda e, g=g, t0=t0: e.activation(out=KT[:, g, t0:t0 + 512], in_=tmp[0][:, :512], func=AF.Copy),
                                 reads=[B_tmp[0]], writes=[B_KT])
                            blk = t0 // 256
                            S.op("dve", lambda e, g=g, blk=blk: e.tensor_reduce(
                                out=ksum[:, g, blk:blk + 2], in_=tmp[0][:, :512].rearrange("p (b k) -> p b k", b=2),
                                axis=AX.X, op=ALU.add), reads=[B_tmp[0]], writes=[B_ksum])
                if j == 0:
                    ck(21)
                wv = [w_next("V"), w_next("V")]
                for ti in range(8):
                    psi = ps_get()
                    for k in range(2):
                        wt, bw = wv[k]
                        mm_acc(psi, ps_t[psi][:, k * 256:(k + 1) * 256],
                               [(uT[:, c, ti * 128:(ti + 1) * 128], wt[:, c, :]) for c in range(NCH)], [bw, B_uT])
                    S.op("act", lambda e, ti=ti, psi=psi: e.activation(out=Vt[:, 8 * j + ti, :], in_=ps_t[psi][:, :512], func=AF.Copy),
                         reads=[ps_b[psi]], writes=[B_V])
                    ps_put(psi)
                if j == 0:
                    ck(22)
                N = 1024
                xp = [sb(f"xpA{i}", [128, 3 + N], F32, stP) for i in range(2)]
                xcf = [sb(f"xcfA{i}", [128, N], F32, stP) for i in range(1)]
                xcb = [sb(f"xcbA{i}", [128, N], BF16, stP) for i in range(1)]
                rr = sb("rrA", [128, N], F32, stP)
                ii = sb("iiA", [128, N], F32, stP)
                hh = sb("hhA", [128, N], F32, stP)
                B_xp = [Buf(), Buf()]
                B_xc, B_xcb, B_r, B_i, B_h = Buf(), Buf(), Buf(), Buf(), Buf()
                for k in range(8):
                    wt, bw = w_next("XR")
                    for gg in range(2):
                        n = 2 * k + gg
                        xq = xp[n % 2]
                        bq = B_xp[n % 2]
                        S.op("pool", lambda e, xq=xq, n=n: e.tensor_copy(out=xq[:, 0:3], in_=halo[:, n, :]), reads=[B_halo], writes=[bq])
                        for tt in range(2):
                            psi = ps_get()
                            mm_acc(psi, ps_t[psi][:, :512],
                                   [(wt[:, c, gg * 128:(gg + 1) * 128], uT[:, c, tt * 512:(tt + 1) * 512]) for c in range(NCH)],
                                   [bw, B_uT])
                            S.op("act", lambda e, xq=xq, tt=tt, psi=psi: e.activation(out=xq[:, 3 + tt * 512:3 + (tt + 1) * 512],
                                                                                   in_=ps_t[psi][:, :512], func=AF.Copy),
                                 reads=[ps_b[psi]], writes=[bq])
                            ps_put(psi)
                        S.op("pool", lambda e, xq=xq, n=n: e.tensor_copy(out=halo[:, n, :], in_=xq[:, N:N + 3]), reads=[bq], writes=[B_halo])
                        rglru((xcf[0], xcb[0], rr, ii, hh), n, xq, N, None, None, N, (bq, B_xc, B_xcb, B_r, B_i, B_h), None)
                        S.op("dve", lambda e, n=n: e.tensor_tensor(out=hcar[:, n:n + 1], in0=hcar[:, n:n + 1], in1=flg[:, j:j + 1], op=ALU.mult),
                             reads=[B_const], writes=[B_car])
                        S.op("dve", lambda e, n=n: e.tensor_tensor_scan(out=hh[:, :N], data0=rr[:, :N], data1=ii[:, :N],
                                                                        initial=hcar[:, n:n + 1], op0=ALU.mult, op1=ALU.add),
                             reads=[B_r, B_i, B_car], writes=[B_h])
                        S.op("dve", lambda e, n=n: e.tensor_copy(out=hcar[:, n:n + 1], in_=hh[:, N - 1:N]), reads=[B_h], writes=[B_car])
                        if j == 0 and n == 0:
                            ck(23)
                S.barrier()
                stP.close()
                if j == 0:
                    ck(3)
            ck(4)

            stB = contextlib.ExitStack()
            ksamp = sb("ksamp", [128, 4, NSMP], BF16, stB, top=True)
            vsamp = sb("vsamp", [8, 4, 512], BF16, stB, top=True)
            qsamp = sb("qsamp", [128, NCH, NSMP], BF16, stB, top=True)
            B_ksamp, B_vsamp, B_qsamp = Buf(), Buf(), Buf()
            stAt = contextlib.ExitStack()
            attnT = sb("attnT", [128, NCH, NT], BF16, stAt, top=True)
            B_attn = Buf("attn")
            stN = contextlib.ExitStack()
            norm_to_uT(stN, NPRE, NT, uT, B_uT, tag="own", tw=128)
            S.barrier()
            stN.close()
            stR = contextlib.ExitStack()
            ropc = sb("ropcO", [128, NT], F32, stR)
            rops = sb("ropsO", [128, NT], F32, stR)
            B_rop = Buf("ropO")
            S.dma("sp", d_rop, ropc[:], ropeC[:, NPRE:NPRE + NT], writes=[B_rop])
            S.dma("sp", d_rop, rops[:], ropeS[:, NPRE:NPRE + NT], writes=[B_rop])
            tap("uT", uT, B_uT)
            st1 = contextlib.ExitStack()
            tmp = [sb(f"tmpB{i}", [128, TT], F32, st1) for i in range(2)]
            B_tmp = [Buf(), Buf()]
            kb = sb("kbB", [128, TT], BF16, st1)
            B_kb = Buf()
            kout = [sb(f"kout{i}", [128, NT], F32, st1) for i in range(2)]
            B_kout = [Buf(), Buf()]
            for k in range(2):
                wt, bw = w_next("K")
                for gg in range(2):
                    g = 2 * k + gg
                    ko, bko = kout[g % 2], B_kout[g % 2]
                    for tt in range(3):
                        c0 = tt * TT
                        psi = ps_get()
                        mm_acc(psi, ps_t[psi][:, :TT],
                               [(wt[:, c, gg * 128:(gg + 1) * 128], uT[:, c, c0:c0 + TT]) for c in range(NCH)], [bw, B_uT])
                        rope_evac(psi, TT, ropc[:, c0:c0 + TT], rops[:, c0:c0 + TT], tmp, B_tmp, kb, B_kb, [B_rop])
                        ps_put(psi)
                        S.op("act", lambda e, ko=ko, c0=c0: e.activation(out=ko[:, c0:c0 + TT], in_=tmp[0][:, :TT], func=AF.Copy),
                             reads=[B_tmp[0]], writes=[bko])
                    S.op("act", lambda e, ko=ko, g=g: e.activation(out=KT[:, g, NPRE:NPRE + NOWN], in_=ko[:, :NOWN], func=AF.Copy),
                         reads=[bko], writes=[B_KT])
                    S.op("act", lambda e, ko=ko, g=g: e.activation(out=ksamp[:, g, :], in_=ko[:, NOWN:NT], func=AF.Copy),
                         reads=[bko], writes=[B_ksamp])
                    S.op("dve", lambda e, ko=ko, g=g: e.tensor_reduce(out=ksum[:, g, 12:16],
                                                                      in_=ko[:, :NOWN].rearrange("p (b k) -> p b k", b=4),
                                                                      axis=AX.X, op=ALU.add), reads=[bko], writes=[B_ksum])
                    S.dma("sp", d_ko[g % 2], o_kT[g * 128:(g + 1) * 128, :], ko[:, :], reads=[bko])
            S.op("dve", lambda e: e.tensor_scalar(out=kmean[:], in0=ksum[:], scalar1=1.0 / 256.0, scalar2=None, op0=ALU.mult),
                 reads=[B_ksum], writes=[B_ksum])
            wv = [w_next("V"), w_next("V")]
            vout = [sb(f"vout{i}", [128, 512], F32, st1) for i in range(2)]
            B_vout = [Buf(), Buf()]
            for ti in range(8):
                psi = ps_get()
                for k in range(2):
                    wt, bw = wv[k]
                    mm_acc(psi, ps_t[psi][:, k * 256:(k + 1) * 256],
                           [(uT[:, c, ti * 128:(ti + 1) * 128], wt[:, c, :]) for c in range(NCH)], [bw, B_uT])
                vo, bvo = vout[ti % 2], B_vout[ti % 2]
                S.op("act", lambda e, vo=vo, psi=psi: e.activation(out=vo[:], in_=ps_t[psi][:, :512], func=AF.Copy),
                     reads=[ps_b[psi]], writes=[bvo])
                ps_put(psi)
                S.op("dve", lambda e, vo=vo, ti=ti: e.tensor_copy(out=Vt[:, 24 + ti, :], in_=vo[:]), reads=[bvo], writes=[B_V])
                S.dma("sp", d_vo[ti % 2], o_v[ti * 128:(ti + 1) * 128, :], vo[:], reads=[bvo])
            for s in range(4):
                psi = ps_get()
                for k in range(2):
                    wt, bw = wv[k]
                    mm_acc(psi, ps_t[psi][0:8, k * 256:(k + 1) * 256],
                           [(uT[:, c, NOWN + 8 * s:NOWN + 8 * s + 8], wt[:, c, :]) for c in range(NCH)], [bw, B_uT])
                vo, bvo = vout[s % 2], B_vout[s % 2]
                S.op("act", lambda e, vo=vo, psi=psi: e.activation(out=vo[0:8, :], in_=ps_t[psi][0:8, :512], func=AF.Copy),
                     reads=[ps_b[psi]], writes=[bvo])
                ps_put(psi)
                S.op("dve", lambda e, vo=vo, s=s: e.tensor_copy(out=vsamp[0:8, s, :], in_=vo[0:8, :]), reads=[bvo], writes=[B_vsamp])
                S.dma("sp", d_vo[s % 2], o_v[NOWN + 8 * s:NOWN + 8 * s + 8, :], vo[0:8, :], reads=[bvo])
            S.barrier()
            st1.close()
            ck(5)
            tap("KT", KT, B_KT)
            tap("V", Vt, B_V)
            tap("kmean", kmean, B_ksum)

            st2 = contextlib.ExitStack()
            tmp = [sb(f"tmpC{i}", [128, TT], F32, st2) for i in range(2)]
            B_tmp = [Buf(), Buf()]
            kb = sb("kbC", [128, TT], BF16, st2)
            B_kb = Buf()
            qb = [sb(f"qb{i}", [128, NT], BF16, st2) for i in range(2)]
            B_qb = [Buf(), Buf()]
            padd = sb("padd", [128, 8, 16], F32, st2)
            pval = sb("pval", [128, 8, 16], F32, st2)
            oind = sb("oind", [128, 8, 16], F32, st2)
            B_msk = Buf()
            S.dma("sp", d_msk, padd[:], pastadd, writes=[B_msk])
            S.dma("sp", d_msk, pval[:], pastval, writes=[B_msk])
            S.dma("sp", d_msk, oind[:], ownind, writes=[B_msk])
            gm = sb("gm", [128, 8, 16], F32, st2)
            m8 = sb("m8", [128, 8, 8], F32, st2)
            sel = sb("sel", [128, 8, 16], F32, st2)
            B_gm, B_m8, B_sel = Buf(), Buf(), Buf()
            negT = [sb(f"negT{i}", [16, NOWN], BF16, st2) for i in range(2)]
            B_negT = [Buf(), Buf()]
            pT = [sb(f"pT{i}", [128, 512], BF16, st2) for i in range(3)]
            B_pT = [Buf(), Buf(), Buf()]
            rden = sb("rden", [128, 256], F32, st2)
            B_rden = Buf()
            pTi = 0
            for k in range(8):
                wt, bw = w_next("Q")
                for gg in range(2):
                    h = 2 * k + gg
                    g = h // 4
                    q_, bq_ = qb[h % 2], B_qb[h % 2]
                    for tt in range(3):
                        c0 = tt * TT
                        psi = ps_get()
                        mm_acc(psi, ps_t[psi][:, :TT],
                               [(wt[:, c, gg * 128:(gg + 1) * 128], uT[:, c, c0:c0 + TT]) for c in range(NCH)], [bw, B_uT])
                        rope_evac(psi, TT, ropc[:, c0:c0 + TT], rops[:, c0:c0 + TT], tmp, B_tmp, kb, B_kb, [B_rop])
                        ps_put(psi)
                        S.op("act", lambda e, q_=q_, c0=c0: e.activation(out=q_[:, c0:c0 + TT], in_=tmp[0][:, :TT], func=AF.Copy),
                             reads=[B_tmp[0]], writes=[bq_])
                    S.op("dve", lambda e, q_=q_, h=h: e.tensor_copy(out=qsamp[:, h, :], in_=q_[:, NOWN:NT]), reads=[bq_], writes=[B_qsamp])
                    pg_ = ps_get()
                    for i in range(8):
                        S.op("pe", lambda e, i=i: e.matmul(ps_t[pg_][:, i * 16:(i + 1) * 16], lhsT=q_[:, i * 128:(i + 1) * 128],
                                                            rhs=kmean[:, g, :], start=True, stop=True),
                             reads=[bq_, B_ksum], writes=[ps_b[pg_]])
                    S.op("dve", lambda e: e.tensor_tensor(out=gm[:].rearrange("p a b -> p (a b)"), in0=ps_t[pg_][:, :128],
                                                         in1=padd[:].rearrange("p a b -> p (a b)"), op=ALU.add),
                         reads=[ps_b[pg_], B_msk], writes=[B_gm])
                    ps_put(pg_)
                    for i in range(8):
                        S.op("dve", lambda e, i=i: e.max(out=m8[:, i, :], in_=gm[:, i, :]), reads=[B_gm], writes=[B_m8])
                    for i in range(8):
                        S.op("dve", lambda e, i=i: e.tensor_scalar(out=sel[:, i, :], in0=gm[:, i, :], scalar1=m8[:, i, 2:3], scalar2=None,
                                                                   op0=ALU.is_ge), reads=[B_gm, B_m8], writes=[B_sel])
                    selv = sel[:].rearrange("p a b -> p (a b)")
                    S.op("dve", lambda e: e.tensor_tensor(out=selv, in0=selv, in1=pval[:].rearrange("p a b -> p (a b)"), op=ALU.mult),
                         reads=[B_msk], writes=[B_sel])
                    S.op("dve", lambda e: e.tensor_tensor(out=selv, in0=selv, in1=oind[:].rearrange("p a b -> p (a b)"), op=ALU.add),
                         reads=[B_msk], writes=[B_sel])
                    S.op("dve", lambda e: e.tensor_scalar(out=selv, in0=selv, scalar1=NEGM, scalar2=-NEGM, op0=ALU.mult, op1=ALU.add),
                         writes=[B_sel])
                    nt_, bnt_ = negT[h % 2], B_negT[h % 2]
                    for half in range(2):
                        ptr = ps_get()
                        for i4 in range(4):
                            i = half * 4 + i4
                            S.op("pe", lambda e, i=i, i4=i4: e.transpose(out=ps_t[ptr][0:16, i4 * 128:(i4 + 1) * 128], in_=sel[:, i, :],
                                                                         identity=identf[:]),
                                 reads=[B_sel, B_const], writes=[ps_b[ptr]])
                        S.op("act", lambda e, half=half, ptr=ptr: e.activation(out=nt_[:, half * 512:(half + 1) * 512],
                                                                               in_=ps_t[ptr][0:16, :512], func=AF.Copy),
                             reads=[ps_b[ptr]], writes=[bnt_])
                        ps_put(ptr)
                    for qbl in range(4):
                        qblk = 12 + qbl
                        q0 = qbl * 256
                        po, pd = ps_get(), ps_get()
                        nkb = qblk + 1
                        for kbk in range(nkb):
                            psS = ps_get()
                            for kt2 in range(2):
                                kcol = kbk * 256 + kt2 * 128
                                prs = [(KT[:, g, kcol:kcol + 128], q_[:, q0:q0 + 256]),
                                       (E_sb[0:16, kbk, :], nt_[0:16, q0:q0 + 256])]
                                if kbk == qblk:
                                    prs.append((identb[:], caus[:, kt2, :]))
                                mm_acc(psS, ps_t[psS][:, kt2 * 256:(kt2 + 1) * 256], prs, [B_KT, bq_, bnt_, B_const])
                            p_, bp_ = pT[pTi % 3], B_pT[pTi % 3]
                            pTi += 1
                            S.op("act", lambda e, p_=p_, psS=psS: e.activation(out=p_[:], in_=ps_t[psS][:, :512], func=AF.Exp, scale=SCALE),
                                 reads=[ps_b[psS]], writes=[bp_])
                            ps_put(psS)
                            for kt2 in range(2):
                                first = (kbk == 0 and kt2 == 0)
                                last = (kbk == nkb - 1 and kt2 == 1)
                                S.op("pe", lambda e, p_=p_, kt2=kt2, first=first, last=last, kbk=kbk: e.matmul(
                                    ps_t[po][:, :256], lhsT=Vt[:, kbk * 2 + kt2, g * 128:(g + 1) * 128], rhs=p_[:, kt2 * 256:(kt2 + 1) * 256],
                                    start=first, stop=last), reads=[B_V, bp_], writes=[ps_b[po]])
                                S.op("pe", lambda e, p_=p_, kt2=kt2, first=first, last=last: e.matmul(
                                    ps_t[pd][:, :256], lhsT=ones[:], rhs=p_[:, kt2 * 256:(kt2 + 1) * 256],
                                    start=first, stop=last), reads=[B_const, bp_], writes=[ps_b[pd]])
                        S.op("dve", lambda e: e.reciprocal(out=rden[:], in_=ps_t[pd][:, :256]), reads=[ps_b[pd]], writes=[B_rden])
                        S.op("dve", lambda e, h=h, q0=q0: e.tensor_tensor(out=attnT[:, h, q0:q0 + 256], in0=ps_t[po][:, :256], in1=rden[:],
                                                                          op=ALU.mult), reads=[ps_b[po], B_rden], writes=[B_attn])
                        ps_put(po)
                        ps_put(pd)
            S.barrier()
            st2.close()
            stR.close()
            ck(6)

            st3 = contextlib.ExitStack()
            KsT = [KT[:, 0:2, :].rearrange("p a b -> p (a b)"), KT[:, 2:4, :].rearrange("p a b -> p (a b)")]
            Vs = [Vt[:, 0:16, :].rearrange("p a (b c) -> p (a b) c", c=128), Vt[:, 16:32, :].rearrange("p a (b c) -> p (a b) c", c=128)]
            B_Ks, B_Vs = [Buf(), Buf()], [Buf(), Buf()]
            kst = [sb(f"kst{i}", [128, 4, 128], F32, st3) for i in range(2)]
            vst = [sb(f"vst{i}", [128, 4, 128], F32, st3) for i in range(2)]
            B_kst, B_vst = [Buf(), Buf()], [Buf(), Buf()]
            d_kst = [S.dsem(f"kst{i}") for i in range(2)]
            d_vst = [S.dsem(f"vst{i}") for i in range(2)]
            ksm = sb("ksm", [128, 32], F32, st3)
            kmn = sb("kmn", [128, 32], BF16, st3)
            B_ksm = Buf()
            gms = sb("gms", [8, 4, 32], F32, st3)
            m8s = sb("m8s", [8, 4, 8], F32, st3)
            sels = sb("sels", [8, 4, 32], F32, st3)
            B_gms, B_m8s, B_sels = Buf(), Buf(), Buf()
            negTs = sb("negTs", [32, 32], BF16, st3)
            B_negTs = Buf()
            qsg = sb("qsg", [128, 32], BF16, st3)
            B_qsg = Buf()
            pTs = [sb(f"pTs{i}", [128, 512], BF16, st3) for i in range(2)]
            B_pTs = [Buf(), Buf()]
            pTn = sb("pTn", [8, 32], BF16, st3)
            B_pTn = Buf()
            rdens = sb("rdens", [128, 32], F32, st3)
            B_rdens = Buf()
            ptb = sb("ptb", [128, 256], I32, st3)
            io4 = sb("io4", [128, 4], I32, st3)
            idx = sb("idx", [128, 4, 256], I32, st3)
            B_idx = Buf("idx")
            S.dma("pool", d_wrg, ptb[:], pt.partition_broadcast(128), writes=[B_idx])
            for g in range(4):
                S.op("pool", lambda e, g=g: e.iota(out=io4[:, g:g + 1], pattern=[[0, 1]], base=g * 128, channel_multiplier=1),
                     writes=[B_idx])
            for g in range(4):
                S.op("pool", lambda e, g=g: e.tensor_scalar(out=idx[:, g, :], in0=ptb[:], scalar1=512, scalar2=io4[:, g:g + 1],
                                                            op0=ALU.mult, op1=ALU.add), writes=[B_idx])
            ci = 0
            pi_ = 0
            for s in range(4):
                for g in range(4):
                    sg = s * 4 + g
                    K_, bK_ = KsT[sg % 2], B_Ks[sg % 2]
                    V_, bV_ = Vs[sg % 2], B_Vs[sg % 2]
                    for ch in range(16):
                        kk = ci % 2
                        ci += 1
                        for pj in range(4):
                            j = s * 64 + ch * 4 + pj
                            S.dmaop("pool", d_kst[kk], lambda e, kk=kk, pj=pj, g=g, j=j: e.indirect_dma_start(
                                out=kst[kk][:, pj, :], out_offset=None, in_=ckT,
                                in_offset=bass.IndirectOffsetOnAxis(ap=idx[:, g, j:j + 1], axis=0)),
                                reads=[B_idx], writes=[B_kst[kk]])
                            S.dmaop("pool", d_vst[kk], lambda e, kk=kk, pj=pj, g=g, j=j: e.indirect_dma_start(
                                out=vst[kk][:, pj, :], out_offset=None, in_=cv,
                                in_offset=bass.IndirectOffsetOnAxis(ap=idx[:, g, j:j + 1], axis=0)),
                                reads=[B_idx], writes=[B_vst[kk]])
                        for b2 in range(2):
                            blk = ch * 2 + b2
                            S.op("act", lambda e, kk=kk, b2=b2, blk=blk, K_=K_: e.activation(
                                out=K_[:, blk * 256:(blk + 1) * 256], in_=kst[kk][:, 2 * b2:2 * b2 + 2, :].rearrange("p a b -> p (a b)"),
                                func=AF.Copy, accum_out=ksm[:, blk:blk + 1]), reads=[B_kst[kk]], writes=[bK_, B_ksm])
                        S.op("dve", lambda e, kk=kk, ch=ch, V_=V_: e.tensor_copy(out=V_[:, ch * 4:(ch + 1) * 4, :], in_=vst[kk][:]),
                             reads=[B_vst[kk]], writes=[bV_])
                    S.op("dve", lambda e: e.tensor_scalar(out=kmn[:], in0=ksm[:], scalar1=1.0 / 256.0, scalar2=None, op0=ALU.mult),
                         reads=[B_ksm], writes=[B_ksm])
                    S.op("dve", lambda e, s=s, g=g: e.tensor_copy(out=qsg[:].rearrange("p (a b) -> p a b", a=4),
                                                                  in_=qsamp[:, 4 * g:4 * g + 4, 8 * s:8 * s + 8]),
                         reads=[B_qsamp], writes=[B_qsg])
                    pg_ = ps_get()
                    for hl in range(4):
                        S.op("pe", lambda e, hl=hl: e.matmul(ps_t[pg_][0:8, hl * 32:(hl + 1) * 32], lhsT=qsg[:, hl * 8:(hl + 1) * 8],
                                                              rhs=kmn[:], start=True, stop=True),
                             reads=[B_qsg, B_ksm], writes=[ps_b[pg_]])
                    S.op("dve", lambda e: e.tensor_copy(out=gms[:].rearrange("p a b -> p (a b)"), in_=ps_t[pg_][0:8, :128]),
                         reads=[ps_b[pg_]], writes=[B_gms])
                    ps_put(pg_)
                    for hl in range(4):
                        S.op("dve", lambda e, hl=hl: e.max(out=m8s[:, hl, :], in_=gms[:, hl, :]), reads=[B_gms], writes=[B_m8s])
                    for hl in range(4):
                        S.op("dve", lambda e, hl=hl: e.tensor_scalar(out=sels[:, hl, :], in0=gms[:, hl, :], scalar1=m8s[:, hl, 2:3],
                                                                     scalar2=None, op0=ALU.is_ge), reads=[B_gms, B_m8s], writes=[B_sels])
                    selv = sels[:].rearrange("p a b -> p (a b)")
                    S.op("dve", lambda e: e.tensor_scalar(out=selv, in0=selv, scalar1=NEGM, scalar2=-NEGM, op0=ALU.mult, op1=ALU.add),
                         writes=[B_sels])
                    ptr = ps_get()
                    for hl in range(4):
                        S.op("pe", lambda e, hl=hl: e.transpose(out=ps_t[ptr][0:32, hl * 8:(hl + 1) * 8], in_=sels[:, hl, :],
                                                                identity=identf[0:8, 0:8]),
                             reads=[B_sels, B_const], writes=[ps_b[ptr]])
                    S.op("act", lambda e: e.activation(out=negTs[:], in_=ps_t[ptr][0:32, 0:32], func=AF.Copy),
                         reads=[ps_b[ptr]], writes=[B_negTs])
                    ps_put(ptr)
                    po, pd = ps_get(), ps_get()
                    for qt in range(4):
                        psS = ps_get()
                        for k16 in range(16):
                            kt = qt * 16 + k16
                            mm_acc(psS, ps_t[psS][:, k16 * 32:(k16 + 1) * 32],
                                   [(K_[:, kt * 128:(kt + 1) * 128], qsg[:]), (E_sb[0:32, kt // 2, :], negTs[:])],
                                   [bK_, B_qsg, B_negTs, B_const])
                        p_, bp_ = pTs[pi_ % 2], B_pTs[pi_ % 2]
                        pi_ += 1
                        S.op("act", lambda e, p_=p_, psS=psS: e.activation(out=p_[:], in_=ps_t[psS][:, :512], func=AF.Exp, scale=SCALE),
                             reads=[ps_b[psS]], writes=[bp_])
                        ps_put(psS)
                        for k16 in range(16):
                            kt = qt * 16 + k16
                            first = (kt == 0)
                            S.op("pe", lambda e, p_=p_, kt=kt, k16=k16, first=first: e.matmul(
                                ps_t[po][:, :32], lhsT=V_[:, kt, :], rhs=p_[:, k16 * 32:(k16 + 1) * 32], start=first, stop=False),
                                reads=[bV_, bp_], writes=[ps_b[po]])
                            S.op("pe", lambda e, p_=p_, k16=k16, first=first: e.matmul(
                                ps_t[pd][:, :32], lhsT=ones[:], rhs=p_[:, k16 * 32:(k16 + 1) * 32], start=first, stop=False),
                                reads=[B_const, bp_], writes=[ps_b[pd]])
                    psn = ps_get()
                    mm_acc(psn, ps_t[psn][0:8, 0:32], [(ksamp[:, g, 8 * s:8 * s + 8], qsg[:]), (identb[0:8, 0:8], causs[:])],
                           [B_ksamp, B_qsg, B_const])
                    S.op("act", lambda e: e.activation(out=pTn[:], in_=ps_t[psn][0:8, 0:32], func=AF.Exp, scale=SCALE),
                         reads=[ps_b[psn]], writes=[B_pTn])
                    ps_put(psn)
                    S.op("pe", lambda e, s=s, g=g: e.matmul(ps_t[po][:, :32], lhsT=vsamp[0:8, s, g * 128:(g + 1) * 128], rhs=pTn[:],
                                                           start=False, stop=True), reads=[B_vsamp, B_pTn], writes=[ps_b[po]])
                    S.op("pe", lambda e: e.matmul(ps_t[pd][:, :32], lhsT=ones[0:8, :], rhs=pTn[:], start=False, stop=True),
                         reads=[B_const, B_pTn], writes=[ps_b[pd]])
                    S.op("dve", lambda e: e.reciprocal(out=rdens[:], in_=ps_t[pd][:, :32]), reads=[ps_b[pd]], writes=[B_rdens])
                    S.op("dve", lambda e, s=s, g=g: e.tensor_tensor(
                        out=attnT[:, 4 * g:4 * g + 4, NOWN + 8 * s:NOWN + 8 * s + 8],
                        in0=ps_t[po][:, :32].rearrange("p (a b) -> p a b", a=4), in1=rdens[:].rearrange("p (a b) -> p a b", a=4),
                        op=ALU.mult), reads=[ps_b[po], B_rdens], writes=[B_attn])
                    ps_put(po)
                    ps_put(pd)
            S.barrier()
            st3.close()
            stA.close()
            ck(7)
            tap("attnT", attnT, B_attn)

            stRn = contextlib.ExitStack()
            rnnT = sb("rnnT", [128, NCH, NT], BF16, stRn, top=True)
            B_rnn = Buf("rnn")
            st4 = contextlib.ExitStack()
            load_wrg(st4)
            N = NT
            xp = [sb(f"xpD{i}", [128, 3 + NOWN], F32, st4) for i in range(2)]
            xps = [sb(f"xpsD{i}", [128, 4, 11], F32, st4) for i in range(2)]
            B_xp = [Buf(), Buf()]
            xcf = sb("xcfD", [128, N], F32, st4)
            xcb = sb("xcbD", [128, N], BF16, st4)
            rr = sb("rrD", [128, N], F32, st4)
            ii = sb("iiD", [128, N], F32, st4)
            hh = sb("hhD", [128, N], F32, st4)
            gl = sb("glD", [128, N], F32, st4)
            B_xc, B_xcb, B_r, B_i, B_h, B_gl = Buf(), Buf(), Buf(), Buf(), Buf(), Buf()
            ohp = sb("ohp", [128, NCH], F32, st4)
            ocp = sb("ocp", [128, NCH, 3], F32, st4)
            ohs = sb("ohs", [128, NCH, 4], F32, st4)
            ocs = sb("ocs", [128, NCH, 4, 3], F32, st4)
            B_oh = Buf()
            for k in range(8):
                wx, bwx = w_next("XR")
                for gg in range(2):
                    n = 2 * k + gg
                    xq, xs_, bq = xp[n % 2], xps[n % 2], B_xp[n % 2]
                    S.op("pool", lambda e, xq=xq, n=n: e.tensor_copy(out=xq[:, 0:3], in_=halo[:, n, :]), reads=[B_halo], writes=[bq])
                    S.op("pool", lambda e, xs_=xs_, n=n: e.tensor_copy(out=xs_[:, :, 0:3], in_=stc[:, n, :, :]), reads=[B_const], writes=[bq])
                    for tt in range(3):
                        c0 = tt * TT
                        psi = ps_get()
                        mm_acc(psi, ps_t[psi][:, :TT],
                               [(wx[:, c, gg * 128:(gg + 1) * 128], uT[:, c, c0:c0 + TT]) for c in range(NCH)], [bwx, B_uT])
                        lo, hi = c0, min(c0 + TT, NOWN)
                        if hi > lo:
                            S.op("act", lambda e, xq=xq, lo=lo, hi=hi, c0=c0, psi=psi: e.activation(
                                out=xq[:, 3 + lo:3 + hi], in_=ps_t[psi][:, lo - c0:hi - c0], func=AF.Copy),
                                reads=[ps_b[psi]], writes=[bq])
                        if c0 + TT > NOWN:
                            S.op("act", lambda e, xs_=xs_, c0=c0, psi=psi: e.activation(
                                out=xs_[:, :, 3:11], in_=ps_t[psi][:, NOWN - c0:NT - c0].rearrange("p (s t) -> p s t", s=4), func=AF.Copy),
                                reads=[ps_b[psi]], writes=[bq])
                        ps_put(psi)
                    S.op("pool", lambda e, xq=xq, n=n: e.tensor_copy(out=ocp[:, n, :], in_=xq[:, NOWN:NOWN + 3]), reads=[bq], writes=[B_oh])
                    S.op("pool", lambda e, xs_=xs_, n=n: e.tensor_copy(out=ocs[:, n, :, :], in_=xs_[:, :, 8:11]), reads=[bq], writes=[B_oh])
                    rglru((xcf, xcb, rr, ii, hh), n, xq, NOWN, xs_, None, N, (bq, B_xc, B_xcb, B_r, B_i, B_h), None)
                    S.op("dve", lambda e, n=n: e.tensor_tensor(out=hcar[:, n:n + 1], in0=hcar[:, n:n + 1], in1=flg[:, 3:4], op=ALU.mult),
                         reads=[B_const], writes=[B_car])
                    S.op("dve", lambda e, n=n: e.tensor_tensor_scan(out=hh[:, :NOWN], data0=rr[:, :NOWN], data1=ii[:, :NOWN],
                                                                    initial=hcar[:, n:n + 1], op0=ALU.mult, op1=ALU.add),
                         reads=[B_r, B_i, B_car], writes=[B_h])
                    for s in range(4):
                        a0 = NOWN + 8 * s
                        S.op("dve", lambda e, n=n, s=s, a0=a0: e.tensor_tensor_scan(out=hh[:, a0:a0 + 8], data0=rr[:, a0:a0 + 8],
                                                                                    data1=ii[:, a0:a0 + 8], initial=sth[:, n, s:s + 1],
                                                                                    op0=ALU.mult, op1=ALU.add),
                             reads=[B_r, B_i, B_const], writes=[B_h])
                    S.op("pool", lambda e, n=n: e.tensor_copy(out=ohp[:, n:n + 1], in_=hh[:, NOWN - 1:NOWN]), reads=[B_h], writes=[B_oh])
                    S.op("pool", lambda e, n=n: e.tensor_copy(out=ohs[:, n, :], in_=hh[:, NOWN:NT].rearrange("p (s t) -> p s t", s=4)[:, :, 7]),
                         reads=[B_h], writes=[B_oh])
                    if gg == 0:
                        wy, bwy = w_next("YG")
                    for tt in range(3):
                        c0 = tt * TT
                        psi = ps_get()
                        mm_acc(psi, ps_t[psi][:, :TT],
                               [(wy[:, c, gg * 128:(gg + 1) * 128], uT[:, c, c0:c0 + TT]) for c in range(NCH)], [bwy, B_uT])
                        S.op("act", lambda e, c0=c0, psi=psi: e.activation(out=gl[:, c0:c0 + TT], in_=ps_t[psi][:, :TT], func=AF.Gelu_apprx_tanh),
                             reads=[ps_b[psi]], writes=[B_gl])
                        ps_put(psi)
                    S.op("dve", lambda e, n=n: e.tensor_tensor(out=rnnT[:, n, :], in0=hh[:, :N], in1=gl[:, :N], op=ALU.mult),
                         reads=[B_h, B_gl], writes=[B_rnn])
            S.dma("sp", d_out, o_hp, ohp[:], reads=[B_oh])
            S.dma("sp", d_out, o_cp, ocp[:], reads=[B_oh])
            S.dma("sp", d_out, o_hs, ohs[:], reads=[B_oh])
            S.dma("sp", d_out, o_cs, ocs[:], reads=[B_oh])
            S.barrier()
            st4.close()
            ck(8)
            tap("rnnT", rnnT, B_rnn)

            stMg = contextlib.ExitStack()
            mergedT = sb("mergedT", [128, NCH, NT], BF16, stMg, top=True)
            B_mrg = Buf("mrg")
            st5 = contextlib.ExitStack()
            sg_ = sb("sg_", [128, 2, NT], F32, st5)
            tm_ = sb("tm_", [128, 2, NT], F32, st5)
            t2_ = sb("t2_", [128, TT], F32, st5)
            B_sg, B_tm, B_t2 = Buf(), Buf(), Buf()
            for k in range(8):
                for half, (gtag, ptag, src, bsrc, pbias) in enumerate((("GA", "PA", rnnT, B_rnn, P_BGA), ("GB", "PB", attnT, B_attn, P_BGB))):
                    wg_, bwg = w_next(gtag)
                    for gg in range(2):
                        c_ = 2 * k + gg
                        for tt in range(3):
                            c0 = tt * TT
                            psi = ps_get()
                            mm_acc(psi, ps_t[psi][:, :TT],
                                   [(wg_[:, c, gg * 128:(gg + 1) * 128], uT[:, c, c0:c0 + TT]) for c in range(NCH)], [bwg, B_uT])
                            S.op("act", lambda e, gg=gg, c0=c0, psi=psi, c_=c_, pbias=pbias: e.activation(
                                out=sg_[:, gg, c0:c0 + TT], in_=ps_t[psi][:, :TT], func=AF.Sigmoid, bias=prm[:, pbias, c_:c_ + 1]),
                                reads=[ps_b[psi], B_const], writes=[B_sg])
                            ps_put(psi)
                    wp_, bwp = w_next(ptag)
                    for gg in range(2):
                        c_ = 2 * k + gg
                        for tt in range(3):
                            c0 = tt * TT
                            psi = ps_get()
                            mm_acc(psi, ps_t[psi][:, :TT],
                                   [(wp_[:, c, gg * 128:(gg + 1) * 128], src[:, c, c0:c0 + TT]) for c in range(NCH)], [bwp, bsrc])
                            if half == 0:
                                S.op("dve", lambda e, gg=gg, c0=c0, psi=psi: e.tensor_tensor(
                                    out=tm_[:, gg, c0:c0 + TT], in0=ps_t[psi][:, :TT], in1=sg_[:, gg, c0:c0 + TT], op=ALU.mult),
                                    reads=[ps_b[psi], B_sg], writes=[B_tm])
                            else:
                                S.op("dve", lambda e, gg=gg, c0=c0, psi=psi: e.tensor_tensor(
                                    out=t2_[:, :TT], in0=ps_t[psi][:, :TT], in1=sg_[:, gg, c0:c0 + TT], op=ALU.mult),
                                    reads=[ps_b[psi], B_sg], writes=[B_t2])
                                S.op("dve", lambda e, gg=gg, c0=c0, c_=c_: e.tensor_tensor(
                                    out=mergedT[:, c_, c0:c0 + TT], in0=t2_[:, :TT], in1=tm_[:, gg, c0:c0 + TT], op=ALU.add),
                                    reads=[B_t2, B_tm], writes=[B_mrg])
                            ps_put(psi)
            tap("mergedT", mergedT, B_mrg)
            S.barrier()
            st5.close()
            stAt.close()
            stRn.close()
            ck(9)

            u2T, B_u2 = uT, B_uT
            x1 = sb("x1", [128, NCH, NT], F32, stB, top=True)
            B_x1 = Buf("x1")
            st6 = contextlib.ExitStack()
            xre = [sb(f"xre{i}", [128, NT], F32, st6) for i in range(2)]
            B_xre = [Buf(), Buf()]
            d_xre = [S.dsem(f"xre{i}") for i in range(2)]
            for k in range(8):
                wt, bw = w_next("WO")
                for gg in range(2):
                    c_ = 2 * k + gg
                    xr_, bxr_ = xre[c_ % 2], B_xre[c_ % 2]
                    S.dma("sp", d_xre[c_ % 2], xr_[:], xT[c_ * 128:(c_ + 1) * 128, NPRE:NPRE + NT], writes=[bxr_])
                    for tt in range(3):
                        c0 = tt * TT
                        psi = ps_get()
                        mm_acc(psi, ps_t[psi][:, :TT],
                               [(wt[:, c, gg * 128:(gg + 1) * 128], mergedT[:, c, c0:c0 + TT]) for c in range(NCH)], [bw, B_mrg])
                        S.op("dve", lambda e, c_=c_, c0=c0, psi=psi, xr_=xr_: e.tensor_tensor(
                            out=x1[:, c_, c0:c0 + TT], in0=ps_t[psi][:, :TT], in1=xr_[:, c0:c0 + TT], op=ALU.add),
                            reads=[ps_b[psi], bxr_], writes=[B_x1])
                        ps_put(psi)
            tap("x1", x1, B_x1)
            S.barrier()
            st6.close()
            stMg.close()
            ck(10)

            def norm_x1(stk, gidx, out_fn, tag):
                def load(t0, w, t, b, d):
                    for c in range(NCH):
                        S.op("pool", lambda e, c=c: e.tensor_copy(out=t[:, c, :w], in_=x1[:, c, t0:t0 + w]), reads=[B_x1], writes=[b])
                rmsnorm(stk, load, NT, gidx, out_fn, tag=tag)

            st7 = contextlib.ExitStack()

            def out_u2(t0, w, xs, bx, rs, br):
                for c in range(NCH):
                    S.op("dve", lambda e, c=c: e.scalar_tensor_tensor(out=u2T[:, c, t0:t0 + w], in0=xs[:, c, :w],
                                                                      scalar=prm[:, P_NMLP, c:c + 1], in1=rs[:, :w],
                                                                      op0=ALU.mult, op1=ALU.mult),
                         reads=[bx, br, B_const], writes=[B_u2])
            norm_x1(st7, P_NMLP, out_u2, "m")
            S.barrier()
            st7.close()
            ck(11)

            st8 = contextlib.ExitStack()
            hff = sb("hff", [128, NCH, NT], BF16, st8)
            B_hff = Buf("hff")
            rl = [sb(f"rl{i}", [128, TT], F32, st8) for i in range(2)]
            B_rl = [Buf(), Buf()]
            ri = 0
            for q in range(4):
                for k in range(8):
                    wt, bw = w_next("F1")
                    for gg in range(2):
                        c_ = 2 * k + gg
                        for tt in range(3):
                            c0 = tt * TT
                            psi = ps_get()
                            mm_acc(psi, ps_t[psi][:, :TT],
                                   [(wt[:, c, gg * 128:(gg + 1) * 128], u2T[:, c, c0:c0 + TT]) for c in range(NCH)], [bw, B_u2])
                            r_, br_ = rl[ri % 2], B_rl[ri % 2]
                            ri += 1
                            S.op("act", lambda e, r_=r_, psi=psi: e.activation(out=r_[:, :TT], in_=ps_t[psi][:, :TT], func=AF.Relu),
                                 reads=[ps_b[psi]], writes=[br_])
                            ps_put(psi)
                            S.op("dve", lambda e, r_=r_, c_=c_, c0=c0: e.tensor_tensor(out=hff[:, c_, c0:c0 + TT], in0=r_[:, :TT], in1=r_[:, :TT],
                                                                                      op=ALU.mult), reads=[br_], writes=[B_hff])
                for k in range(8):
                    wt, bw = w_next("F2")
                    for gg in range(2):
                        c_ = 2 * k + gg
                        for tt in range(3):
                            c0 = tt * TT
                            psi = ps_get()
                            mm_acc(psi, ps_t[psi][:, :TT],
                                   [(wt[:, c, gg * 128:(gg + 1) * 128], hff[:, c, c0:c0 + TT]) for c in range(NCH)], [bw, B_hff])
                            S.op("dve", lambda e, c_=c_, c0=c0, psi=psi: e.tensor_tensor(
                                out=x1[:, c_, c0:c0 + TT], in0=ps_t[psi][:, :TT], in1=x1[:, c_, c0:c0 + TT], op=ALU.add),
                                reads=[ps_b[psi]], writes=[B_x1])
                            ps_put(psi)
            S.barrier()
            st8.close()
            stU.close()
            ck(12)

            st9 = contextlib.ExitStack()
            yo = [sb(f"yo{i}", [128, NCH, 256], F32, st9) for i in range(2)]
            B_yo = [Buf(), Buf()]
            yi = [0]

            def out_y(t0, w, xs, bx, rs, br):
                y_, by_ = yo[yi[0] % 2], B_yo[yi[0] % 2]
                dy_ = d_yo[yi[0] % 2]
                yi[0] += 1
                for c in range(NCH):
                    S.op("dve", lambda e, c=c: e.scalar_tensor_tensor(out=y_[:, c, :w], in0=xs[:, c, :w],
                                                                      scalar=prm[:, P_NFIN, c:c + 1], in1=rs[:, :w],
                                                                      op0=ALU.mult, op1=ALU.mult),
                         reads=[bx, br, B_const], writes=[by_])
                S.dma("sp", dy_, o_yT.rearrange("(c p) t -> p c t", p=128)[:, :, t0:t0 + w], y_[:, :, :w], reads=[by_])
            norm_x1(st9, P_NFIN, out_y, "f")
            S.barrier()
            st9.close()
            stB.close()
        except _Stop:
            pass
        S.waitfor("sp", reads=[])
        S.barrier()
    print("SBUF arena peak bytes/partition:", arena.peak, "of", arena.hi - arena.lo)
    return nc


def _rope_tables(pos):
    half = 64
    inv = (np.float32(10000.0) ** (-np.arange(half, dtype=np.float32) * np.float32(2.0 / 128))).astype(np.float32)
    ang = pos.astype(np.float32)[:, None] * inv[None, :]
    cos = np.cos(ang).astype(np.float32).T
    sin = np.sin(ang).astype(np.float32).T
    C = np.concatenate([cos, cos], axis=0)
    Ssig = np.concatenate([-sin, sin], axis=0)
    return np.ascontiguousarray(C), np.ascontiguousarray(Ssig)


def _consts():
    bf = ml_dtypes.bfloat16
    E = np.zeros((32, 32, 128), np.float32)
    for j in range(32):
        E[j, j, :] = 1.0
    ident = np.eye(128, dtype=np.float32)
    swap = np.zeros((128, 128), np.float32)
    for m in range(128):
        swap[(m + 64) % 128, m] = 1.0
    caus = np.zeros((128, 2, 256), np.float32)
    for kt2 in range(2):
        kk = kt2 * 128 + np.arange(128)[:, None]
        qq = np.arange(256)[None, :]
        caus[:, kt2, :] = np.where(kk <= qq, 0.0, -NEGM)
    causs = np.zeros((8, 32), np.float32)
    for t2 in range(8):
        for hl in range(4):
            for t in range(8):
                causs[t2, hl * 8 + t] = 0.0 if t2 <= t else -NEGM
    return {"c_E": E.astype(bf), "c_identb": ident.astype(bf), "c_identf": ident, "c_ones": np.ones((128, 128), bf),
            "c_swap": swap.astype(bf), "c_caus": caus.astype(bf), "c_causs": causs.astype(bf)}


def _fm(v):
    return np.ascontiguousarray(np.asarray(v, np.float32).reshape(NCH, 128).T)


_NC_CACHE = {}


def _weight_groups(ws):
    out = np.empty((NGRP, 128, NCH, GW), np.float32)
    g = 0
    for w in ws:
        w = np.asarray(w, np.float32)[0]
        rb, nb = w.shape[0] // D, w.shape[1] // GW
        v = w.reshape(rb, NCH, 128, nb, GW).transpose(0, 3, 2, 1, 4)
        out[g:g + rb * nb] = v.reshape(rb * nb, 128, NCH, GW)
        g += rb * nb
    assert g == NGRP
    return out.reshape(NGRP * 128, NCH * GW)


def _cache_layouts(cache_k, cache_v):
    f32 = np.float32
    ck = np.ascontiguousarray(np.asarray(cache_k, f32)[0].transpose(0, 2, 3, 1)).reshape(-1, 128)
    cv = np.ascontiguousarray(np.asarray(cache_v, f32)[0].transpose(0, 2, 1, 3)).reshape(-1, 128)
    return ck, cv


def make_in_maps(x_prompt, x_sample, cache_k, cache_v, state_h, state_conv, page_table,
                 norm_mix, w_in, b_gate, conv_w, conv_b, w_rg_a, b_rg_a, w_rg_x, b_rg_x,
                 lru_lambda, w_proj_a, w_proj_b, w_out, norm_mlp, w_ff1, w_ff2, norm_final, cores=range(8), pool_fn=None):
    f32 = np.float32
    x_prompt = np.asarray(x_prompt, f32)
    x_sample = np.asarray(x_sample, f32)
    consts = _consts()
    pp = np.zeros((128, NPRM, NCH), f32)
    pp[:, P_NMIX] = _fm(norm_mix[0]); pp[:, P_NMLP] = _fm(norm_mlp[0]); pp[:, P_NFIN] = _fm(norm_final)
    for j in range(4):
        pp[:, P_CW0 + j] = _fm(conv_w[0, j])
    pp[:, P_CB] = _fm(conv_b[0]); pp[:, P_BRA] = _fm(b_rg_a[0]); pp[:, P_BRX] = _fm(b_rg_x[0])
    pp[:, P_LAM] = _fm(lru_lambda[0]); pp[:, P_BGA] = _fm(b_gate[0, :D]); pp[:, P_BGB] = _fm(b_gate[0, D:])
    shared = {
        "wall": _weight_groups([w_in, w_proj_a, w_proj_b, w_out, w_ff1, w_ff2]),
        "w_rga": np.ascontiguousarray(np.asarray(w_rg_a, f32)[0].transpose(1, 0, 2)).reshape(128, 2048),
        "w_rgx": np.ascontiguousarray(np.asarray(w_rg_x, f32)[0].transpose(1, 0, 2)).reshape(128, 2048),
        "pp": pp,
    }
    if pool_fn is None:
        shared["ckT"], shared["cv"] = _cache_layouts(cache_k, cache_v)
    shared.update(consts)
    xpT = [np.ascontiguousarray(x_prompt[b].T) for b in range(2)]
    in_maps = []
    for c in cores:
        b, r = c // 4, c % 4
        pad = (3 - r) * 1024
        xT = np.zeros((D, NCOL), f32)
        xT[:, pad:NPRE + NOWN] = xpT[b][:, :1024 * (r + 1)]
        xs = x_sample[4 * c:4 * c + 4].reshape(32, D)
        xT[:, NPRE + NOWN:] = xs.T
        pos = np.concatenate([np.maximum(np.arange(NPRE + NOWN) - pad, 0), 8192 + np.tile(np.arange(8), 4)]).astype(np.int64)
        C, Ssig = _rope_tables(pos)
        flags = np.ones((128, 4), f32)
        flags[:, 3 - r] = 0.0
        padb = pad // 256
        pastadd = np.zeros((128, 8, 16), f32)
        pastval = np.zeros((128, 8, 16), f32)
        ownind = np.zeros((128, 8, 16), f32)
        for i in range(8):
            qb_ = 12 + i // 2
            for kbk in range(16):
                ok = (kbk >= padb) and (kbk < qb_)
                pastval[:, i, kbk] = 1.0 if ok else 0.0
                pastadd[:, i, kbk] = 0.0 if ok else -1e30
            ownind[:, i, qb_] = 1.0
        m = dict(shared)
        ptc = np.ascontiguousarray(np.asarray(page_table, np.int32)[4 * c:4 * c + 4].reshape(1, 256))
        if pool_fn is not None:
            ptc, m["ckT"], m["cv"] = pool_fn(ptc)
        m.update({
            "xT": xT, "ropeC": C, "ropeS": Ssig, "flags": flags,
            "st_h": np.ascontiguousarray(np.asarray(state_h, f32)[0, 4 * c:4 * c + 4].reshape(4, NCH, 128).transpose(2, 1, 0)),
            "st_c": np.ascontiguousarray(np.asarray(state_conv, f32)[0, 4 * c:4 * c + 4].reshape(4, 3, NCH, 128).transpose(3, 2, 0, 1)),
            "pt": ptc,
            "pastadd": pastadd, "pastval": pastval, "ownind": ownind,
        })
        in_maps.append(m)
    return in_maps


def kernel(x_prompt, x_sample, cache_k, cache_v, state_h, state_conv, page_table,
           norm_mix, w_in, b_gate, conv_w, conv_b, w_rg_a, b_rg_a, w_rg_x, b_rg_x,
           lru_lambda, w_proj_a, w_proj_b, w_out, norm_mlp, w_ff1, w_ff2, norm_final):
    f32 = np.float32
    in_maps = make_in_maps(x_prompt, x_sample, cache_k, cache_v, state_h, state_conv, page_table,
                           norm_mix, w_in, b_gate, conv_w, conv_b, w_rg_a, b_rg_a, w_rg_x, b_rg_x,
                           lru_lambda, w_proj_a, w_proj_b, w_out, norm_mlp, w_ff1, w_ff2, norm_final)
    if "nc" not in _NC_CACHE:
        _NC_CACHE["nc"] = build_nc()
    nc = _NC_CACHE["nc"]
    res = run_bass_kernel_spmd(nc, in_maps, core_ids=list(range(8)))
    R = res.results
    y_prompt = np.zeros((2, SEQ, D), f32)
    y_sample = np.zeros((32, 8, D), f32)
    k_prompt = np.zeros((1, 2, SEQ, 4, 128), f32)
    v_prompt = np.zeros((1, 2, SEQ, 4, 128), f32)
    h_prompt = np.zeros((1, 2, D), f32)
    conv_prompt = np.zeros((1, 2, 3, D), f32)
    k_sample = np.zeros((1, 32, 8, 4, 128), f32)
    v_sample = np.zeros((1, 32, 8, 4, 128), f32)
    h_sample = np.zeros((1, 32, D), f32)
    conv_sample = np.zeros((1, 32, 3, D), f32)
    for c in range(8):
        b, r = c // 4, c % 4
        o = R[c]
        yT = np.asarray(o["o_yT"])
        y_prompt[b, 1024 * r:1024 * (r + 1)] = yT[:, :NOWN].T
        y_sample[4 * c:4 * c + 4] = yT[:, NOWN:].T.reshape(4, 8, D)
        kT = np.asarray(o["o_kT"])
        k_prompt[0, b, 1024 * r:1024 * (r + 1)] = kT[:, :NOWN].T.reshape(NOWN, 4, 128)
        k_sample[0, 4 * c:4 * c + 4] = kT[:, NOWN:].T.reshape(4, 8, 4, 128)
        v = np.asarray(o["o_v"])
        v_prompt[0, b, 1024 * r:1024 * (r + 1)] = v[:NOWN].reshape(NOWN, 4, 128)
        v_sample[0, 4 * c:4 * c + 4] = v[NOWN:].reshape(4, 8, 4, 128)
        if r == 3:
            h_prompt[0, b] = np.asarray(o["o_hp"]).T.reshape(D)
            conv_prompt[0, b] = np.asarray(o["o_cp"]).transpose(2, 1, 0).reshape(3, D)
        h_sample[0, 4 * c:4 * c + 4] = np.asarray(o["o_hs"]).transpose(2, 1, 0).reshape(4, D)
        conv_sample[0, 4 * c:4 * c + 4] = np.asarray(o["o_cs"]).transpose(2, 3, 1, 0).reshape(4, 3, D)
    if DEBUG:
        kernel.debug = R
    return (y_prompt, y_sample, k_prompt, v_prompt, h_prompt, conv_prompt, k_sample, v_sample, h_sample, conv_sample)
```

```python
import contextlib
import numpy as np
import ml_dtypes
import concourse.bass as bass
import concourse.mybir as mybir
from concourse.bass_utils import run_bass_kernel_spmd

F32 = mybir.dt.float32
BF16 = mybir.dt.bfloat16
I32 = mybir.dt.int32
AF = mybir.ActivationFunctionType
ALU = mybir.AluOpType
AX = mybir.AxisListType

D = 2048
NCH = 16
SEQ = 4096
NPRE = 3072
NOWN = 1024
NSMP = 32
NT = NOWN + NSMP
TT = 352
NCOL = NPRE + NT
N_PHYS = 2560
GW = 256
NSLOT = 4
NGRP = 132
NEGM = 30000.0
SCALE = 128 ** -0.5
O_K, O_V, O_XR, O_YG, O_GA, O_GB = 2048, 2560, 3072, 5120, 7168, 9216
P_NMIX, P_NMLP, P_NFIN, P_CW0, P_CB, P_BRA, P_BRX, P_LAM, P_BGA, P_BGB = 0, 1, 2, 3, 7, 8, 9, 10, 11, 12
NPRM = 13

DEBUG = False


class _Stop(Exception):
    pass


STOP = [0]
PREF = [3]


class Buf:
    __slots__ = ("w", "r", "name", "excl")

    def __init__(self, name="", excl=False):
        self.w = {}
        self.r = {}
        self.name = name
        self.excl = excl


DT_SIZE = {F32: 4, BF16: 2, I32: 4}


class Arena:
    def __init__(self, lo, hi):
        self.free_list = [(lo, hi)]
        self.used = {}
        self.peak = 0
        self.hi = hi
        self.lo = lo

    def alloc(self, nbytes, name="", top=False):
        nbytes = (nbytes + 63) // 64 * 64
        order = range(len(self.free_list) - 1, -1, -1) if top else range(len(self.free_list))
        for i in order:
            a, b = self.free_list[i]
            if b - a >= nbytes:
                if b - a == nbytes:
                    self.free_list.pop(i)
                    off = a
                elif top:
                    self.free_list[i] = (a, b - nbytes)
                    off = b - nbytes
                else:
                    self.free_list[i] = (a + nbytes, b)
                    off = a
                self.used[off] = nbytes
                tot = sum(self.used.values())
                self.peak = max(self.peak, tot)
                return off
        raise MemoryError(f"SBUF arena exhausted allocating {name} ({nbytes}B); used={sum(self.used.values())} free={self.free_list}")

    def free(self, off):
        n = self.used.pop(off)
        self.free_list.append((off, off + n))
        self.free_list.sort()
        merged = []
        for a, b in self.free_list:
            if merged and merged[-1][1] == a:
                merged[-1] = (merged[-1][0], b)
            else:
                merged.append((a, b))
        self.free_list = merged


class Sched:
    CH = 16000

    def __init__(self, nc, es):
        self.nc, self.es = nc, es
        self.E = {"pe": nc.tensor, "act": nc.scalar, "dve": nc.vector, "pool": nc.gpsimd, "sp": nc.sync}
        self.cnt = {e: 0 for e in self.E}
        self.sems = {e: [] for e in self.E}
        self.seen = {e: {} for e in self.E}
        self.dsems = []

    def _sem(self, e, ep):
        while len(self.sems[e]) <= ep:
            self.sems[e].append(self.es.enter_context(self.nc.semaphore(f"s_{e}_{len(self.sems[e])}")))
        return self.sems[e][ep]

    def dsem(self, name):
        d = [self.es.enter_context(self.nc.semaphore("d_" + name)), 0]
        self.dsems.append(d)
        return d

    def _wait(self, e, tok):
        eng = self.E[e]
        if tok[0] == "dma":
            _, d, val = tok
            key = ("dma", id(d))
            if self.seen[e].get(key, 0) >= val:
                return
            eng.wait_ge(d[0], val)
            self.seen[e][key] = val
        else:
            p, seq = tok
            if p == e and e == "pe":
                return
            ep = (seq - 1) // self.CH
            v = (seq - 1) % self.CH + 1
            cur = self.seen[e].get(p, (-1, 0))
            if cur[0] > ep or (cur[0] == ep and cur[1] >= v):
                return
            eng.wait_ge(self._sem(p, ep), v)
            self.seen[e][p] = (ep, v)

    def _deps(self, reads, writes):
        deps = []
        for b in reads:
            deps += list(b.w.values())
            if b.excl:
                deps += list(b.r.values())
        for b in writes:
            deps += list(b.w.values())
            deps += list(b.r.values())
        return deps

    @staticmethod
    def _key(tok):
        return ("dma", id(tok[1])) if tok[0] == "dma" else tok[0]

    def _commit(self, tok, reads, writes):
        k = self._key(tok)
        for b in writes:
            b.w = {k: tok}
            b.r = {}
        for b in reads:
            if b not in writes:
                b.r[k] = tok

    def waitfor(self, e, reads=(), writes=()):
        for t in self._deps(reads, writes):
            self._wait(e, t)

    def op(self, e, fn, reads=(), writes=()):
        for t in self._deps(reads, writes):
            self._wait(e, t)
        inst = fn(self.E[e])
        self.cnt[e] += 1
        seq = self.cnt[e]
        inst.then_inc(self._sem(e, (seq - 1) // self.CH), 1)
        tok = (e, seq)
        self._commit(tok, reads, writes)
        return tok

    def dma(self, q, d, out, in_, reads=(), writes=(), **kw):
        for t in self._deps(reads, writes):
            self._wait(q, t)
        inst = self.E[q].dma_start(out=out, in_=in_, **kw)
        d[1] += 16
        inst.then_inc(d[0], 16)
        tok = ("dma", d, d[1])
        self._commit(tok, reads, writes)
        return tok

    def dmaop(self, q, d, fn, reads=(), writes=()):
        for t in self._deps(reads, writes):
            self._wait(q, t)
        inst = fn(self.E[q])
        d[1] += 16
        inst.then_inc(d[0], 16)
        tok = ("dma", d, d[1])
        self._commit(tok, reads, writes)
        return tok

    def barrier(self):
        for e in self.E:
            for p in self.E:
                if self.cnt[p] > 0:
                    self._wait(e, (p, self.cnt[p]))
            for d in self.dsems:
                if d[1] > 0:
                    self._wait(e, ("dma", d, d[1]))


def build_nc(n_phys=N_PHYS):
    nc = bass.Bass("TRN2", target_bir_lowering=False)
    es = contextlib.ExitStack()

    def din(name, shape, dt=F32):
        return nc.dram_tensor(name, list(shape), dt, kind="ExternalInput").ap()

    def dout(name, shape, dt=F32):
        return nc.dram_tensor(name, list(shape), dt, kind="ExternalOutput").ap()

    xT = din("xT", [D, NCOL])
    ropeC = din("ropeC", [128, NCOL])
    ropeS = din("ropeS", [128, NCOL])
    wall = din("wall", [NGRP * 128, NCH * GW])
    w_rga = din("w_rga", [128, 16 * 128])
    w_rgx = din("w_rgx", [128, 16 * 128])
    pp = din("pp", [128, NPRM, NCH])
    st_h = din("st_h", [128, NCH, 4])
    st_c = din("st_c", [128, NCH, 4, 3])
    flags = din("flags", [128, 4])
    pt = din("pt", [1, 256], I32)
    ckT = din("ckT", [n_phys * 256, 256])
    cv = din("cv", [n_phys * 256, 256])
    pastadd = din("pastadd", [128, 8, 16])
    pastval = din("pastval", [128, 8, 16])
    ownind = din("ownind", [128, 8, 16])
    c_E = din("c_E", [32, 32, 128], BF16)
    c_identb = din("c_identb", [128, 128], BF16)
    c_identf = din("c_identf", [128, 128])
    c_ones = din("c_ones", [128, 128], BF16)
    c_swap = din("c_swap", [128, 128], BF16)
    c_caus = din("c_caus", [128, 2, 256], BF16)
    c_causs = din("c_causs", [8, 32], BF16)

    o_yT = dout("o_yT", [D, NT])
    o_kT = dout("o_kT", [512, NT])
    o_v = dout("o_v", [NT, 512])
    o_hp = dout("o_hp", [128, NCH])
    o_cp = dout("o_cp", [128, NCH, 3])
    o_hs = dout("o_hs", [128, NCH, 4])
    o_cs = dout("o_cs", [128, NCH, 4, 3])
    dbg = {}
    if DEBUG:
        dbg["uT"] = dout("g_uT", [128, NCH, NT], BF16)
        dbg["KT"] = dout("g_KT", [128, 4, SEQ], BF16)
        dbg["V"] = dout("g_V", [128, 32, 512], BF16)
        dbg["kmean"] = dout("g_kmean", [128, 4, 16], BF16)
        dbg["attnT"] = dout("g_attnT", [128, NCH, NT], BF16)
        dbg["rnnT"] = dout("g_rnnT", [128, NCH, NT], BF16)
        dbg["mergedT"] = dout("g_mergedT", [128, NCH, NT], BF16)
        dbg["x1"] = dout("g_x1", [128, NCH, NT])

    S = Sched(nc, es)

    def ck(k):
        if STOP[0] == k:
            S.barrier()
            raise _Stop()

    _uniq = [0]
    arena = Arena(16512, 229344)

    def sb(name, shape, dt, stack=None, top=False):
        _uniq[0] += 1
        nb = 1
        for d_ in shape[1:]:
            nb *= d_
        nb *= DT_SIZE[dt]
        off = arena.alloc(nb, name, top=top)
        t = nc.alloc_sbuf_tensor_at(f"{name}_{_uniq[0]}", list(shape), dt, offset=off)
        (stack or es).callback(arena.free, off)
        return t

    with es:
        try:
            E_sb = sb("E_sb", [32, 32, 128], BF16)
            identb = sb("identb", [128, 128], BF16)
            identf = sb("identf", [128, 128], F32)
            ones = sb("ones", [128, 128], BF16)
            swapm = sb("swapm", [128, 128], BF16)
            caus = sb("caus", [128, 2, 256], BF16)
            causs = sb("causs", [8, 32], BF16)
            prm = sb("prm", [128, NPRM, NCH], F32)
            cvec = sb("cvec", [128, NCH], F32)
            flg = sb("flg", [128, 4], F32)
            hcar = sb("hcar", [128, NCH], F32)
            halo = sb("halo", [128, NCH, 3], F32)
            sth = sb("sth", [128, NCH, 4], F32)
            stc = sb("stc", [128, NCH, 4, 3], F32)
            wsl = [sb(f"wsl{i}", [128, NCH, GW], BF16) for i in range(NSLOT)]
            B_const = Buf("const")
            B_w = [Buf(f"w{i}") for i in range(NSLOT)]
            d_w = [S.dsem(f"w{i}") for i in range(NSLOT)]
            d_misc = S.dsem("misc")
            d_wrg = S.dsem("wrg")
            d_out = S.dsem("out")
            d_rop = S.dsem("rop")
            d_msk = S.dsem("msk")
            d_ko = [S.dsem(f"ko{i}") for i in range(2)]
            d_vo = [S.dsem(f"vo{i}") for i in range(2)]
            d_yo = [S.dsem(f"yo{i}") for i in range(2)]
            B_car = Buf("car")
            B_halo = Buf("halo")

            ps_t = [es.enter_context(nc.psum_tensor(f"ps{i}", [128, 512], F32)) for i in range(8)]
            ps_b = [Buf(f"ps{i}", excl=True) for i in range(8)]
            ps_free = list(range(8))

            def ps_get():
                i = ps_free.pop(0)
                return i

            def ps_put(i):
                ps_free.append(i)

            for dst, src in [(E_sb, c_E), (identb, c_identb), (identf, c_identf), (ones, c_ones), (swapm, c_swap),
                             (caus, c_caus), (causs, c_causs), (prm, pp), (flg, flags), (sth, st_h), (stc, st_c)]:
                S.dma("sp", d_misc, dst[:], src, writes=[B_const])
            wrg = {}
            B_wrg = Buf("wrg")

            def load_wrg(stk):
                wrg["a"] = sb("wrga", [128, 16, 128], BF16, stk)
                wrg["x"] = sb("wrgx", [128, 16, 128], BF16, stk)
                S.dma("pool", d_wrg, wrg["a"][:].rearrange("c n d -> c (n d)"), w_rga, writes=[B_wrg])
                S.dma("pool", d_wrg, wrg["x"][:].rearrange("c n d -> c (n d)"), w_rgx, writes=[B_wrg])
            S.op("act", lambda e: e.activation(out=cvec[:], in_=prm[:, P_LAM, :], func=AF.Exp, scale=-1.0),
                 reads=[B_const], writes=[B_car])
            S.op("dve", lambda e: e.tensor_scalar(out=cvec[:], in0=cvec[:], scalar1=1.0, scalar2=None, op0=ALU.add),
                 writes=[B_car])
            S.op("act", lambda e: e.activation(out=cvec[:], in_=cvec[:], func=AF.Ln), writes=[B_car])
            S.op("dve", lambda e: e.tensor_scalar(out=cvec[:], in0=cvec[:], scalar1=-8.0, scalar2=None, op0=ALU.mult),
                 writes=[B_car])
            S.op("dve", lambda e: e.memset(hcar[:], 0.0), writes=[B_car])
            S.op("dve", lambda e: e.memset(halo[:], 0.0), writes=[B_halo])

            def wgrp(g):
                return wall[g * 128:(g + 1) * 128, :].rearrange("p (a n) -> p a n", a=2)

            w_in, w_pa, w_pb, w_o, w_f1, w_f2 = "w_in", "w_pa", "w_pb", "w_o", "w_f1", "w_f2"
            G_BASE = {"w_in": 0, "w_pa": 44, "w_pb": 52, "w_o": 60, "w_f1": 68, "w_f2": 100}

            def wview(w, r0, c0):
                return wgrp(G_BASE[w] + (r0 // D) * 8 + c0 // GW)

            plan = []
            for j in range(3):
                for k in range(2):
                    plan.append(("K", wview(w_in, 0, O_K + GW * k)))
                for k in range(2):
                    plan.append(("V", wview(w_in, 0, O_V + GW * k)))
                for k in range(8):
                    plan.append(("XR", wview(w_in, 0, O_XR + GW * k)))
            for k in range(2):
                plan.append(("K", wview(w_in, 0, O_K + GW * k)))
            for k in range(2):
                plan.append(("V", wview(w_in, 0, O_V + GW * k)))
            for k in range(8):
                plan.append(("Q", wview(w_in, 0, GW * k)))
            for k in range(8):
                plan.append(("XR", wview(w_in, 0, O_XR + GW * k)))
                plan.append(("YG", wview(w_in, 0, O_YG + GW * k)))
            for k in range(8):
                plan.append(("GA", wview(w_in, 0, O_GA + GW * k)))
                plan.append(("PA", wview(w_pa, 0, GW * k)))
                plan.append(("GB", wview(w_in, 0, O_GB + GW * k)))
                plan.append(("PB", wview(w_pb, 0, GW * k)))
            for k in range(8):
                plan.append(("WO", wview(w_o, 0, GW * k)))
            for q in range(4):
                for k in range(8):
                    plan.append(("F1", wview(w_f1, 0, D * q + GW * k)))
                for k in range(8):
                    plan.append(("F2", wview(w_f2, D * q, GW * k)))
            wst = {"issued": 0, "cons": 0}

            def w_issue(upto):
                while wst["issued"] < min(upto, len(plan)):
                    i = wst["issued"]
                    s = i % NSLOT
                    S.dma("pool", d_w[s], wsl[s][:].rearrange("p c n -> p (c n)").rearrange("p (a n) -> p a n", a=2), plan[i][1],
                          writes=[B_w[s]])
                    wst["issued"] += 1

            def w_next(tag):
                i = wst["cons"]
                assert plan[i][0] == tag, (i, plan[i][0], tag)
                w_issue(i + PREF[0])
                wst["cons"] += 1
                s = i % NSLOT
                return wsl[s], B_w[s]

            def mm_acc(psi, out_ap, pairs, reads):
                def fn(e):
                    n = len(pairs)
                    ins = None
                    for i, (l, r) in enumerate(pairs):
                        ins = e.matmul(out_ap, lhsT=l, rhs=r, start=(i == 0), stop=(i == n - 1))
                    return ins
                return S.op("pe", fn, reads=reads, writes=[ps_b[psi]])

            def tap(name, t, b):
                if DEBUG and name in dbg:
                    S.dma("sp", d_out, dbg[name], t[:], reads=[b])
                    S.barrier()

            xT_v = xT.rearrange("(c p) t -> p c t", p=128)

            def rmsnorm(stk, src_load, ncols, gidx, out_fn, tw=256, tag=""):
                xst = [sb(f"xst{tag}{i}", [128, NCH, tw], F32, stk) for i in range(2)]
                B_xst = [Buf(), Buf()]
                d_x = [S.dsem(f"x{tag}{i}") for i in range(2)]
                sqb = [sb(f"sqb{tag}{i}", [128, tw], BF16, stk) for i in range(2)]
                B_sq = [Buf(), Buf()]
                rstd = [sb(f"rstd{tag}{i}", [128, tw], F32, stk) for i in range(2)]
                B_rs = [Buf(), Buf()]
                ntile = (ncols + tw - 1) // tw
                for it in range(ntile):
                    t0 = it * tw
                    w = min(tw, ncols - t0)
                    k = it % 2
                    src_load(t0, w, xst[k], B_xst[k], d_x[k])
                    psi = ps_get()
                    for c in range(NCH):
                        kk = c % 2
                        S.op("act", lambda e, c=c, kk=kk: e.activation(out=sqb[kk][:, :w], in_=xst[k][:, c, :w], func=AF.Square),
                             reads=[B_xst[k]], writes=[B_sq[kk]])
                        S.op("pe", lambda e, c=c, kk=kk: e.matmul(ps_t[psi][:, :w], lhsT=ones[:], rhs=sqb[kk][:, :w],
                                                                     start=(c == 0), stop=(c == NCH - 1)),
                             reads=[B_sq[kk], B_const], writes=[ps_b[psi]])
                    S.op("dve", lambda e: e.tensor_scalar(out=rstd[k][:, :w], in0=ps_t[psi][:, :w], scalar1=1.0 / D, scalar2=1e-6,
                                                         op0=ALU.mult, op1=ALU.add), reads=[ps_b[psi]], writes=[B_rs[k]])
                    ps_put(psi)
                    S.op("act", lambda e: e.activation(out=rstd[k][:, :w], in_=rstd[k][:, :w], func=AF.Sqrt), writes=[B_rs[k]])
                    S.op("dve", lambda e: e.reciprocal(out=rstd[k][:, :w], in_=rstd[k][:, :w]), writes=[B_rs[k]])
                    out_fn(t0, w, xst[k], B_xst[k], rstd[k], B_rs[k])

            def norm_to_uT(stk, col0, ncols, uT, B_uT, tag, tw=256):
                def load(t0, w, t, b, d):
                    S.dma("sp", d, t[:, :, :w], xT_v[:, :, col0 + t0:col0 + t0 + w], writes=[b])

                def outf(t0, w, xs, bx, rs, br):
                    for c in range(NCH):
                        S.op("dve", lambda e, c=c: e.scalar_tensor_tensor(out=uT[:, c, t0:t0 + w], in0=xs[:, c, :w],
                                                                          scalar=prm[:, P_NMIX, c:c + 1], in1=rs[:, :w],
                                                                          op0=ALU.mult, op1=ALU.mult),
                             reads=[bx, br, B_const], writes=[B_uT])
                rmsnorm(stk, load, ncols, P_NMIX, outf, tw=tw, tag=tag)

            def rope_evac(psi, w, cosap, sinap, tmp, B_tmp, kb, B_kb, reads_tab):
                S.op("act", lambda e: e.activation(out=kb[:, :w], in_=ps_t[psi][:, :w], func=AF.Copy),
                     reads=[ps_b[psi]], writes=[B_kb])
                ck(2131)
                p2 = ps_get()
                S.op("pe", lambda e: e.matmul(ps_t[p2][:, :w], lhsT=swapm[:], rhs=kb[:, :w], start=True, stop=True),
                     reads=[B_kb, B_const], writes=[ps_b[p2]])
                ck(2132)
                S.op("dve", lambda e: e.tensor_tensor(out=tmp[0][:, :w], in0=ps_t[psi][:, :w], in1=cosap, op=ALU.mult),
                     reads=[ps_b[psi]] + reads_tab, writes=[B_tmp[0]])
                ck(2133)
                S.op("dve", lambda e: e.tensor_tensor(out=tmp[1][:, :w], in0=ps_t[p2][:, :w], in1=sinap, op=ALU.mult),
                     reads=[ps_b[p2]] + reads_tab, writes=[B_tmp[1]])
                ps_put(p2)
                ck(2134)
                S.op("dve", lambda e: e.tensor_tensor(out=tmp[0][:, :w], in0=tmp[0][:, :w], in1=tmp[1][:, :w], op=ALU.add),
                     reads=[B_tmp[1]], writes=[B_tmp[0]])

            def rglru(stk_bufs, n, xp_prompt, npr, xps, xc, N, B, h_out):
                (xcf, xcb, rr, ii, hh) = stk_bufs
                (B_xp, B_xc, B_xcb, B_r, B_i, B_h) = B
                cw = lambda j: prm[:, P_CW0 + j, n:n + 1]
                S.op("dve", lambda e: e.tensor_scalar(out=xcf[:, :npr], in0=xp_prompt[:, 0:npr], scalar1=cw(0),
                                                     scalar2=prm[:, P_CB, n:n + 1], op0=ALU.mult, op1=ALU.add),
                     reads=[B_xp, B_const], writes=[B_xc])
                for j in range(1, 4):
                    S.op("dve", lambda e, j=j: e.scalar_tensor_tensor(out=xcf[:, :npr], in0=xp_prompt[:, j:j + npr], scalar=cw(j),
                                                                      in1=xcf[:, :npr], op0=ALU.mult, op1=ALU.add),
                         reads=[B_xp], writes=[B_xc])
                if xps is not None:
                    xcs = xcf[:, npr:npr + 32].rearrange("p (s t) -> p s t", s=4)
                    S.op("dve", lambda e: e.tensor_scalar(out=xcs, in0=xps[:, :, 0:8], scalar1=cw(0),
                                                         scalar2=prm[:, P_CB, n:n + 1], op0=ALU.mult, op1=ALU.add),
                         reads=[B_xp, B_const], writes=[B_xc])
                    for j in range(1, 4):
                        S.op("dve", lambda e, j=j: e.scalar_tensor_tensor(out=xcs, in0=xps[:, :, j:j + 8], scalar=cw(j),
                                                                          in1=xcs, op0=ALU.mult, op1=ALU.add),
                             reads=[B_xp], writes=[B_xc])
                S.op("act", lambda e: e.activation(out=xcb[:, :N], in_=xcf[:, :N], func=AF.Copy), reads=[B_xc], writes=[B_xcb])
                tiles = [(t0, min(512, N - t0)) for t0 in range(0, N, 512)]
                for (wg, dst, bdst, pb) in ((wrg["a"], rr, B_r, P_BRA), (wrg["x"], ii, B_i, P_BRX)):
                    for (t0, w) in tiles:
                        psi = ps_get()
                        S.op("pe", lambda e, wg=wg, t0=t0, w=w, psi=psi: e.matmul(ps_t[psi][:, :w], lhsT=wg[:, n, :], rhs=xcb[:, t0:t0 + w],
                                                                               start=True, stop=True),
                             reads=[B_xcb, B_wrg], writes=[ps_b[psi]])
                        S.op("act", lambda e, dst=dst, t0=t0, w=w, psi=psi, pb=pb: e.activation(
                            out=dst[:, t0:t0 + w], in_=ps_t[psi][:, :w], func=AF.Sigmoid, bias=prm[:, pb, n:n + 1]),
                             reads=[ps_b[psi], B_const], writes=[bdst])
                        ps_put(psi)
                S.op("act", lambda e: e.activation(out=rr[:, :N], in_=rr[:, :N], func=AF.Exp, scale=cvec[:, n:n + 1]),
                     reads=[B_car], writes=[B_r])
                S.op("pool", lambda e: e.tensor_tensor(out=hh[:, :N], in0=rr[:, :N], in1=rr[:, :N], op=ALU.mult),
                     reads=[B_r], writes=[B_h])
                S.op("dve", lambda e: e.tensor_scalar(out=hh[:, :N], in0=hh[:, :N], scalar1=-1.0, scalar2=1.0,
                                                     op0=ALU.mult, op1=ALU.add), writes=[B_h])
                S.op("dve", lambda e: e.tensor_scalar(out=hh[:, :N], in0=hh[:, :N], scalar1=0.0, scalar2=None, op0=ALU.max),
                     writes=[B_h])
                S.op("act", lambda e: e.activation(out=hh[:, :N], in_=hh[:, :N], func=AF.Sqrt), writes=[B_h])
                S.op("pool", lambda e: e.tensor_tensor(out=ii[:, :N], in0=ii[:, :N], in1=hh[:, :N], op=ALU.mult),
                     reads=[B_h], writes=[B_i])
                S.op("pool", lambda e: e.tensor_tensor(out=ii[:, :N], in0=ii[:, :N], in1=xcf[:, :N], op=ALU.mult),
                     reads=[B_xc], writes=[B_i])
                return

            ck(1)
            stU = contextlib.ExitStack()
            uT = sb("uT", [128, NCH, NT], BF16, stU, top=True)
            stA = contextlib.ExitStack()
            KT = sb("KT", [128, 4, SEQ], BF16, stA)
            Vt = sb("Vt", [128, 32, 512], BF16, stA)
            ksum = sb("ksum", [128, 4, 16], F32, stA)
            kmean = sb("kmean", [128, 4, 16], BF16, stA)
            B_KT, B_V, B_uT, B_ksum = Buf("KT"), Buf("V"), Buf("uT"), Buf("ksum")

            for j in range(3):
                stP = contextlib.ExitStack()
                col0 = 1024 * j
                load_wrg(stP)
                ropc = sb("ropc", [128, 1024], F32, stP)
                rops = sb("rops", [128, 1024], F32, stP)
                B_rop = Buf("rop")
                S.dma("sp", d_rop, ropc[:], ropeC[:, col0:col0 + 1024], writes=[B_rop])
                S.dma("sp", d_rop, rops[:], ropeS[:, col0:col0 + 1024], writes=[B_rop])
                stN = contextlib.ExitStack()
                norm_to_uT(stN, col0, 1024, uT, B_uT, tag=f"a{j}")
                S.barrier()
                stN.close()
                if j == 0:
                    ck(2)
                tmp = [sb(f"tmpA{i}", [128, 512], F32, stP) for i in range(2)]
                B_tmp = [Buf(), Buf()]
                kb = sb("kbA", [128, 512], BF16, stP)
                B_kb = Buf()
                for k in range(2):
                    wt, bw = w_next("K")
                    if j == 0 and k == 0:
                        ck(211)
                    for gg in range(2):
                        g = 2 * k + gg
                        for tt in range(2):
                            psi = ps_get()
                            mm_acc(psi, ps_t[psi][:, :512],
                                   [(wt[:, c, gg * 128:(gg + 1) * 128], uT[:, c, tt * 512:(tt + 1) * 512]) for c in range(NCH)],
                                   [bw, B_uT])
                            if j == 0 and g == 0 and tt == 0:
                                ck(212)
                            rope_evac(psi, 512, ropc[:, tt * 512:(tt + 1) * 512], rops[:, tt * 512:(tt + 1) * 512],
                                      tmp, B_tmp, kb, B_kb, [B_rop])
                            if j == 0 and g == 0 and tt == 0:
                                ck(213)
                            ps_put(psi)
                            t0 = col0 + tt * 512
                            S.op("act", lambda e, g=g, t0=t0: e.activation(out=KT[:, g, t0:t0 + 512], in_=tmp[0][:, :512], func=AF.Copy),
                                 reads=[B_tmp[0]], writes=[B_KT])
                            blk = t0 // 256
                            S.op("dve", lambda e, g=g, blk=blk: e.tensor_reduce(
                                out=ksum[:, g, blk:blk + 2], in_=tmp[0][:, :512].rearrange("p (b k) -> p b k", b=2),
                                axis=AX.X, op=ALU.add), reads=[B_tmp[0]], writes=[B_ksum])
                if j == 0:
                    ck(21)
                wv = [w_next("V"), w_next("V")]
                for ti in range(8):
                    psi = ps_get()
                    for k in range(2):
                        wt, bw = wv[k]
                        mm_acc(psi, ps_t[psi][:, k * 256:(k + 1) * 256],
                               [(uT[:, c, ti * 128:(ti + 1) * 128], wt[:, c, :]) for c in range(NCH)], [bw, B_uT])
                    S.op("act", lambda e, ti=ti, psi=psi: e.activation(out=Vt[:, 8 * j + ti, :], in_=ps_t[psi][:, :512], func=AF.Copy),
                         reads=[ps_b[psi]], writes=[B_V])
                    ps_put(psi)
                if j == 0:
                    ck(22)
                N = 1024
                xp = [sb(f"xpA{i}", [128, 3 + N], F32, stP) for i in range(2)]
                xcf = [sb(f"xcfA{i}", [128, N], F32, stP) for i in range(1)]
                xcb = [sb(f"xcbA{i}", [128, N], BF16, stP) for i in range(1)]
                rr = sb("rrA", [128, N], F32, stP)
                ii = sb("iiA", [128, N], F32, stP)
                hh = sb("hhA", [128, N], F32, stP)
                B_xp = [Buf(), Buf()]
                B_xc, B_xcb, B_r, B_i, B_h = Buf(), Buf(), Buf(), Buf(), Buf()
                for k in range(8):
                    wt, bw = w_next("XR")
                    for gg in range(2):
                        n = 2 * k + gg
                        xq = xp[n % 2]
                        bq = B_xp[n % 2]
                        S.op("pool", lambda e, xq=xq, n=n: e.tensor_copy(out=xq[:, 0:3], in_=halo[:, n, :]), reads=[B_halo], writes=[bq])
                        for tt in range(2):
                            psi = ps_get()
                            mm_acc(psi, ps_t[psi][:, :512],
                                   [(wt[:, c, gg * 128:(gg + 1) * 128], uT[:, c, tt * 512:(tt + 1) * 512]) for c in range(NCH)],
                                   [bw, B_uT])
                            S.op("act", lambda e, xq=xq, tt=tt, psi=psi: e.activation(out=xq[:, 3 + tt * 512:3 + (tt + 1) * 512],
                                                                                   in_=ps_t[psi][:, :512], func=AF.Copy),
                                 reads=[ps_b[psi]], writes=[bq])
                            ps_put(psi)
                        S.op("pool", lambda e, xq=xq, n=n: e.tensor_copy(out=halo[:, n, :], in_=xq[:, N:N + 3]), reads=[bq], writes=[B_halo])
                        rglru((xcf[0], xcb[0], rr, ii, hh), n, xq, N, None, None, N, (bq, B_xc, B_xcb, B_r, B_i, B_h), None)
                        S.op("dve", lambda e, n=n: e.tensor_tensor(out=hcar[:, n:n + 1], in0=hcar[:, n:n + 1], in1=flg[:, j:j + 1], op=ALU.mult),
                             reads=[B_const], writes=[B_car])
                        S.op("dve", lambda e, n=n: e.tensor_tensor_scan(out=hh[:, :N], data0=rr[:, :N], data1=ii[:, :N],
                                                                        initial=hcar[:, n:n + 1], op0=ALU.mult, op1=ALU.add),
                             reads=[B_r, B_i, B_car], writes=[B_h])
                        S.op("dve", lambda e, n=n: e.tensor_copy(out=hcar[:, n:n + 1], in_=hh[:, N - 1:N]), reads=[B_h], writes=[B_car])
                        if j == 0 and n == 0:
                            ck(23)
                S.barrier()
                stP.close()
                if j == 0:
                    ck(3)
            ck(4)

            stB = contextlib.ExitStack()
            ksamp = sb("ksamp", [128, 4, NSMP], BF16, stB, top=True)
            vsamp = sb("vsamp", [8, 4, 512], BF16, stB, top=True)
            qsamp = sb("qsamp", [128, NCH, NSMP], BF16, stB, top=True)
            B_ksamp, B_vsamp, B_qsamp = Buf(), Buf(), Buf()
            stAt = contextlib.ExitStack()
            attnT = sb("attnT", [128, NCH, NT], BF16, stAt, top=True)
            B_attn = Buf("attn")
            stN = contextlib.ExitStack()
            norm_to_uT(stN, NPRE, NT, uT, B_uT, tag="own", tw=128)
            S.barrier()
            stN.close()
            stR = contextlib.ExitStack()
            ropc = sb("ropcO", [128, NT], F32, stR)
            rops = sb("ropsO", [128, NT], F32, stR)
            B_rop = Buf("ropO")
            S.dma("sp", d_rop, ropc[:], ropeC[:, NPRE:NPRE + NT], writes=[B_rop])
            S.dma("sp", d_rop, rops[:], ropeS[:, NPRE:NPRE + NT], writes=[B_rop])
            tap("uT", uT, B_uT)
            st1 = contextlib.ExitStack()
            tmp = [sb(f"tmpB{i}", [128, TT], F32, st1) for i in range(2)]
            B_tmp = [Buf(), Buf()]
            kb = sb("kbB", [128, TT], BF16, st1)
            B_kb = Buf()
            kout = [sb(f"kout{i}", [128, NT], F32, st1) for i in range(2)]
            B_kout = [Buf(), Buf()]
            for k in range(2):
                wt, bw = w_next("K")
                for gg in range(2):
                    g = 2 * k + gg
                    ko, bko = kout[g % 2], B_kout[g % 2]
                    for tt in range(3):
                        c0 = tt * TT
                        psi = ps_get()
                        mm_acc(psi, ps_t[psi][:, :TT],
                               [(wt[:, c, gg * 128:(gg + 1) * 128], uT[:, c, c0:c0 + TT]) for c in range(NCH)], [bw, B_uT])
                        rope_evac(psi, TT, ropc[:, c0:c0 + TT], rops[:, c0:c0 + TT], tmp, B_tmp, kb, B_kb, [B_rop])
                        ps_put(psi)
                        S.op("act", lambda e, ko=ko, c0=c0: e.activation(out=ko[:, c0:c0 + TT], in_=tmp[0][:, :TT], func=AF.Copy),
                             reads=[B_tmp[0]], writes=[bko])
                    S.op("act", lambda e, ko=ko, g=g: e.activation(out=KT[:, g, NPRE:NPRE + NOWN], in_=ko[:, :NOWN], func=AF.Copy),
                         reads=[bko], writes=[B_KT])
                    S.op("act", lambda e, ko=ko, g=g: e.activation(out=ksamp[:, g, :], in_=ko[:, NOWN:NT], func=AF.Copy),
                         reads=[bko], writes=[B_ksamp])
                    S.op("dve", lambda e, ko=ko, g=g: e.tensor_reduce(out=ksum[:, g, 12:16],
                                                                      in_=ko[:, :NOWN].rearrange("p (b k) -> p b k", b=4),
                                                                      axis=AX.X, op=ALU.add), reads=[bko], writes=[B_ksum])
                    S.dma("sp", d_ko[g % 2], o_kT[g * 128:(g + 1) * 128, :], ko[:, :], reads=[bko])
            S.op("dve", lambda e: e.tensor_scalar(out=kmean[:], in0=ksum[:], scalar1=1.0 / 256.0, scalar2=None, op0=ALU.mult),
                 reads=[B_ksum], writes=[B_ksum])
            wv = [w_next("V"), w_next("V")]
            vout = [sb(f"vout{i}", [128, 512], F32, st1) for i in range(2)]
            B_vout = [Buf(), Buf()]
            for ti in range(8):
                psi = ps_get()
                for k in range(2):
                    wt, bw = wv[k]
                    mm_acc(psi, ps_t[psi][:, k * 256:(k + 1) * 256],
                           [(uT[:, c, ti * 128:(ti + 1) * 128], wt[:, c, :]) for c in range(NCH)], [bw, B_uT])
                vo, bvo = vout[ti % 2], B_vout[ti % 2]
                S.op("act", lambda e, vo=vo, psi=psi: e.activation(out=vo[:], in_=ps_t[psi][:, :512], func=AF.Copy),
                     reads=[ps_b[psi]], writes=[bvo])
                ps_put(psi)
                S.op("dve", lambda e, vo=vo, ti=ti: e.tensor_copy(out=Vt[:, 24 + ti, :], in_=vo[:]), reads=[bvo], writes=[B_V])
                S.dma("sp", d_vo[ti % 2], o_v[ti * 128:(ti + 1) * 128, :], vo[:], reads=[bvo])
            for s in range(4):
                psi = ps_get()
                for k in range(2):
                    wt, bw = wv[k]
                    mm_acc(psi, ps_t[psi][0:8, k * 256:(k + 1) * 256],
                           [(uT[:, c, NOWN + 8 * s:NOWN + 8 * s + 8], wt[:, c, :]) for c in range(NCH)], [bw, B_uT])
                vo, bvo = vout[s % 2], B_vout[s % 2]
                S.op("act", lambda e, vo=vo, psi=psi: e.activation(out=vo[0:8, :], in_=ps_t[psi][0:8, :512], func=AF.Copy),
                     reads=[ps_b[psi]], writes=[bvo])
                ps_put(psi)
                S.op("dve", lambda e, vo=vo, s=s: e.tensor_copy(out=vsamp[0:8, s, :], in_=vo[0:8, :]), reads=[bvo], writes=[B_vsamp])
                S.dma("sp", d_vo[s % 2], o_v[NOWN + 8 * s:NOWN + 8 * s + 8, :], vo[0:8, :], reads=[bvo])
            S.barrier()
            st1.close()
            ck(5)
            tap("KT", KT, B_KT)
            tap("V", Vt, B_V)
            tap("kmean", kmean, B_ksum)

            st2 = contextlib.ExitStack()
            tmp = [sb(f"tmpC{i}", [128, TT], F32, st2) for i in range(2)]
            B_tmp = [Buf(), Buf()]
            kb = sb("kbC", [128, TT], BF16, st2)
            B_kb = Buf()
            qb = [sb(f"qb{i}", [128, NT], BF16, st2) for i in range(2)]
            B_qb = [Buf(), Buf()]
            padd = sb("padd", [128, 8, 16], F32, st2)
            pval = sb("pval", [128, 8, 16], F32, st2)
            oind = sb("oind", [128, 8, 16], F32, st2)
            B_msk = Buf()
            S.dma("sp", d_msk, padd[:], pastadd, writes=[B_msk])
            S.dma("sp", d_msk, pval[:], pastval, writes=[B_msk])
            S.dma("sp", d_msk, oind[:], ownind, writes=[B_msk])
            gm = sb("gm", [128, 8, 16], F32, st2)
            m8 = sb("m8", [128, 8, 8], F32, st2)
            sel = sb("sel", [128, 8, 16], F32, st2)
            B_gm, B_m8, B_sel = Buf(), Buf(), Buf()
            negT = [sb(f"negT{i}", [16, NOWN], BF16, st2) for i in range(2)]
            B_negT = [Buf(), Buf()]
            pT = [sb(f"pT{i}", [128, 512], BF16, st2) for i in range(3)]
            B_pT = [Buf(), Buf(), Buf()]
            rden = sb("rden", [128, 256], F32, st2)
            B_rden = Buf()
            pTi = 0
            for k in range(8):
                wt, bw = w_next("Q")
                for gg in range(2):
                    h = 2 * k + gg
                    g = h // 4
                    q_, bq_ = qb[h % 2], B_qb[h % 2]
                    for tt in range(3):
                        c0 = tt * TT
                        psi = ps_get()
                        mm_acc(psi, ps_t[psi][:, :TT],
                               [(wt[:, c, gg * 128:(gg + 1) * 128], uT[:, c, c0:c0 + TT]) for c in range(NCH)], [bw, B_uT])
                        rope_evac(psi, TT, ropc[:, c0:c0 + TT], rops[:, c0:c0 + TT], tmp, B_tmp, kb, B_kb, [B_rop])
                        ps_put(psi)
                        S.op("act", lambda e, q_=q_, c0=c0: e.activation(out=q_[:, c0:c0 + TT], in_=tmp[0][:, :TT], func=AF.Copy),
                             reads=[B_tmp[0]], writes=[bq_])
                    S.op("dve", lambda e, q_=q_, h=h: e.tensor_copy(out=qsamp[:, h, :], in_=q_[:, NOWN:NT]), reads=[bq_], writes=[B_qsamp])
                    pg_ = ps_get()
                    for i in range(8):
                        S.op("pe", lambda e, i=i: e.matmul(ps_t[pg_][:, i * 16:(i + 1) * 16], lhsT=q_[:, i * 128:(i + 1) * 128],
                                                            rhs=kmean[:, g, :], start=True, stop=True),
                             reads=[bq_, B_ksum], writes=[ps_b[pg_]])
                    S.op("dve", lambda e: e.tensor_tensor(out=gm[:].rearrange("p a b -> p (a b)"), in0=ps_t[pg_][:, :128],
                                                         in1=padd[:].rearrange("p a b -> p (a b)"), op=ALU.add),
                         reads=[ps_b[pg_], B_msk], writes=[B_gm])
                    ps_put(pg_)
                    for i in range(8):
                        S.op("dve", lambda e, i=i: e.max(out=m8[:, i, :], in_=gm[:, i, :]), reads=[B_gm], writes=[B_m8])
                    for i in range(8):
                        S.op("dve", lambda e, i=i: e.tensor_scalar(out=sel[:, i, :], in0=gm[:, i, :], scalar1=m8[:, i, 2:3], scalar2=None,
                                                                   op0=ALU.is_ge), reads=[B_gm, B_m8], writes=[B_sel])
                    selv = sel[:].rearrange("p a b -> p (a b)")
                    S.op("dve", lambda e: e.tensor_tensor(out=selv, in0=selv, in1=pval[:].rearrange("p a b -> p (a b)"), op=ALU.mult),
                         reads=[B_msk], writes=[B_sel])
                    S.op("dve", lambda e: e.tensor_tensor(out=selv, in0=selv, in1=oind[:].rearrange("p a b -> p (a b)"), op=ALU.add),
                         reads=[B_msk], writes=[B_sel])
                    S.op("dve", lambda e: e.tensor_scalar(out=selv, in0=selv, scalar1=NEGM, scalar2=-NEGM, op0=ALU.mult, op1=ALU.add),
                         writes=[B_sel])
                    nt_, bnt_ = negT[h % 2], B_negT[h % 2]
                    for half in range(2):
                        ptr = ps_get()
                        for i4 in range(4):
                            i = half * 4 + i4
                            S.op("pe", lambda e, i=i, i4=i4: e.transpose(out=ps_t[ptr][0:16, i4 * 128:(i4 + 1) * 128], in_=sel[:, i, :],
                                                                         identity=identf[:]),
                                 reads=[B_sel, B_const], writes=[ps_b[ptr]])
                        S.op("act", lambda e, half=half, ptr=ptr: e.activation(out=nt_[:, half * 512:(half + 1) * 512],
                                                                               in_=ps_t[ptr][0:16, :512], func=AF.Copy),
                             reads=[ps_b[ptr]], writes=[bnt_])
                        ps_put(ptr)
                    for qbl in range(4):
                        qblk = 12 + qbl
                        q0 = qbl * 256
                        po, pd = ps_get(), ps_get()
                        nkb = qblk + 1

                        def emit_S(kbk, qblk=qblk, q0=q0):
                            nonlocal pTi
                            psS = ps_get()
                            for kt2 in range(2):
                                kcol = kbk * 256 + kt2 * 128
                                prs = [(KT[:, g, kcol:kcol + 128], q_[:, q0:q0 + 256]),
                                       (E_sb[0:16, kbk, :], nt_[0:16, q0:q0 + 256])]
                                if kbk == qblk:
                                    prs.append((identb[:], caus[:, kt2, :]))
                                mm_acc(psS, ps_t[psS][:, kt2 * 256:(kt2 + 1) * 256], prs, [B_KT, bq_, bnt_, B_const])
                            p_, bp_ = pT[pTi % 3], B_pT[pTi % 3]
                            pTi += 1
                            S.op("act", lambda e, p_=p_, psS=psS: e.activation(out=p_[:], in_=ps_t[psS][:, :512], func=AF.Exp, scale=SCALE),
                                 reads=[ps_b[psS]], writes=[bp_])
                            ps_put(psS)
                            return p_, bp_

                        def emit_PV(kbk, p_, bp_, nkb=nkb, po=po, pd=pd):
                            for kt2 in range(2):
                                first = (kbk == 0 and kt2 == 0)
                                last = (kbk == nkb - 1 and kt2 == 1)
                                S.op("pe", lambda e, p_=p_, kt2=kt2, first=first, last=last, kbk=kbk: e.matmul(
                                    ps_t[po][:, :256], lhsT=Vt[:, kbk * 2 + kt2, g * 128:(g + 1) * 128], rhs=p_[:, kt2 * 256:(kt2 + 1) * 256],
                                    start=first, stop=last), reads=[B_V, bp_], writes=[ps_b[po]])
                                S.op("pe", lambda e, p_=p_, kt2=kt2, first=first, last=last: e.matmul(
                                    ps_t[pd][:, :256], lhsT=ones[:], rhs=p_[:, kt2 * 256:(kt2 + 1) * 256],
                                    start=first, stop=last), reads=[B_const, bp_], writes=[ps_b[pd]])

                        pend = None
                        for kbk in range(nkb):
                            cur = emit_S(kbk)
                            if pend is not None:
                                emit_PV(*pend)
                            pend = (kbk,) + cur
                        emit_PV(*pend)
                        S.op("dve", lambda e: e.reciprocal(out=rden[:], in_=ps_t[pd][:, :256]), reads=[ps_b[pd]], writes=[B_rden])
                        S.op("dve", lambda e, h=h, q0=q0: e.tensor_tensor(out=attnT[:, h, q0:q0 + 256], in0=ps_t[po][:, :256], in1=rden[:],
                                                                          op=ALU.mult), reads=[ps_b[po], B_rden], writes=[B_attn])
                        ps_put(po)
                        ps_put(pd)
            S.barrier()
            st2.close()
            stR.close()
            ck(6)

            st3 = contextlib.ExitStack()
            KsT = [KT[:, 0:2, :].rearrange("p a b -> p (a b)"), KT[:, 2:4, :].rearrange("p a b -> p (a b)")]
            Vs = [Vt[:, 0:16, :].rearrange("p a (b c) -> p (a b) c", c=128), Vt[:, 16:32, :].rearrange("p a (b c) -> p (a b) c", c=128)]
            B_Ks, B_Vs = [Buf(), Buf()], [Buf(), Buf()]
            kst = [sb(f"kst{i}", [128, 4, 256], F32, st3) for i in range(2)]
            vst = [sb(f"vst{i}", [128, 4, 256], F32, st3) for i in range(2)]
            B_kst, B_vst = [Buf(), Buf()], [Buf(), Buf()]
            d_kst = [S.dsem(f"kst{i}") for i in range(2)]
            d_vst = [S.dsem(f"vst{i}") for i in range(2)]
            ksp = sb("ksp", [128, 2, 64], F32, st3)
            ksm = sb("ksm", [128, 2, 32], F32, st3)
            kmn = sb("kmn", [128, 2, 32], BF16, st3)
            B_ksp, B_kmn = [Buf(), Buf()], [Buf(), Buf()]
            gms = sb("gms", [8, 4, 32], F32, st3)
            m8s = sb("m8s", [8, 4, 8], F32, st3)
            sels = sb("sels", [8, 4, 32], F32, st3)
            B_gms, B_m8s, B_sels = Buf(), Buf(), Buf()
            negTs = sb("negTs", [32, 32], BF16, st3)
            B_negTs = Buf()
            qsg = sb("qsg", [128, 32], BF16, st3)
            B_qsg = Buf()
            pTs = [sb(f"pTs{i}", [128, 512], BF16, st3) for i in range(2)]
            B_pTs = [Buf(), Buf()]
            pTn = sb("pTn", [8, 32], BF16, st3)
            B_pTn = Buf()
            rdens = sb("rdens", [128, 32], F32, st3)
            B_rdens = Buf()
            ptb = sb("ptb", [128, 256], I32, st3)
            io2 = sb("io2", [128, 2], I32, st3)
            idx = sb("idx", [128, 2, 256], I32, st3)
            B_idx = Buf("idx")
            S.dma("pool", d_wrg, ptb[:], pt.partition_broadcast(128), writes=[B_idx])
            for gp in range(2):
                S.op("pool", lambda e, gp=gp: e.iota(out=io2[:, gp:gp + 1], pattern=[[0, 1]], base=gp, channel_multiplier=2),
                     writes=[B_idx])
            for gp in range(2):
                S.op("pool", lambda e, gp=gp: e.tensor_scalar(out=idx[:, gp, :], in0=ptb[:], scalar1=256, scalar2=io2[:, gp:gp + 1],
                                                              op0=ALU.mult, op1=ALU.add), writes=[B_idx])
            ci = 0
            pi_ = 0
            for s in range(4):
                for g in range(4):
                    gp, g2 = g // 2, g % 2
                    K_, bK_ = KsT[g2], B_Ks[g2]
                    V_, bV_ = Vs[g2], B_Vs[g2]
                    if g2 == 0:
                        for ch in range(16):
                            kk = ci % 2
                            ci += 1
                            for pj in range(4):
                                j = s * 64 + ch * 4 + pj
                                S.dmaop("pool", d_kst[kk], lambda e, kk=kk, pj=pj, gp=gp, j=j: e.indirect_dma_start(
                                    out=kst[kk][:, pj, :], out_offset=None, in_=ckT,
                                    in_offset=bass.IndirectOffsetOnAxis(ap=idx[:, gp, j:j + 1], axis=0)),
                                    reads=[B_idx], writes=[B_kst[kk]])
                                S.dmaop("pool", d_vst[kk], lambda e, kk=kk, pj=pj, gp=gp, j=j: e.indirect_dma_start(
                                    out=vst[kk][:, pj, :], out_offset=None, in_=cv,
                                    in_offset=bass.IndirectOffsetOnAxis(ap=idx[:, gp, j:j + 1], axis=0)),
                                    reads=[B_idx], writes=[B_vst[kk]])
                            for h2 in range(2):
                                for pj in range(4):
                                    page = ch * 4 + pj
                                    S.op("act", lambda e, kk=kk, pj=pj, page=page, h2=h2: e.activation(
                                        out=KsT[h2][:, page * 128:(page + 1) * 128], in_=kst[kk][:, pj, h2 * 128:(h2 + 1) * 128],
                                        func=AF.Copy, accum_out=ksp[:, h2, page:page + 1]), reads=[B_kst[kk]], writes=[B_Ks[h2], B_ksp[h2]])
                                S.op("dve", lambda e, kk=kk, ch=ch, h2=h2: e.tensor_copy(
                                    out=Vs[h2][:, ch * 4:(ch + 1) * 4, :], in_=vst[kk][:, :, h2 * 128:(h2 + 1) * 128]),
                                    reads=[B_vst[kk]], writes=[B_Vs[h2]])
                        for h2 in range(2):
                            S.op("dve", lambda e, h2=h2: e.tensor_reduce(out=ksm[:, h2, :], in_=ksp[:, h2, :].rearrange("p (b k) -> p b k", k=2),
                                                                       axis=AX.X, op=ALU.add), reads=[B_ksp[h2]], writes=[B_kmn[h2]])
                            S.op("dve", lambda e, h2=h2: e.tensor_scalar(out=kmn[:, h2, :], in0=ksm[:, h2, :], scalar1=1.0 / 256.0, scalar2=None,
                                                                       op0=ALU.mult), writes=[B_kmn[h2]])
                    S.op("dve", lambda e, s=s, g=g: e.tensor_copy(out=qsg[:].rearrange("p (a b) -> p a b", a=4),
                                                                  in_=qsamp[:, 4 * g:4 * g + 4, 8 * s:8 * s + 8]),
                         reads=[B_qsamp], writes=[B_qsg])
                    pg_ = ps_get()
                    for hl in range(4):
                        S.op("pe", lambda e, hl=hl: e.matmul(ps_t[pg_][0:8, hl * 32:(hl + 1) * 32], lhsT=qsg[:, hl * 8:(hl + 1) * 8],
                                                              rhs=kmn[:, g2, :], start=True, stop=True),
                             reads=[B_qsg, B_kmn[g2]], writes=[ps_b[pg_]])
                    S.op("dve", lambda e: e.tensor_copy(out=gms[:].rearrange("p a b -> p (a b)"), in_=ps_t[pg_][0:8, :128]),
                         reads=[ps_b[pg_]], writes=[B_gms])
                    ps_put(pg_)
                    for hl in range(4):
                        S.op("dve", lambda e, hl=hl: e.max(out=m8s[:, hl, :], in_=gms[:, hl, :]), reads=[B_gms], writes=[B_m8s])
                    for hl in range(4):
                        S.op("dve", lambda e, hl=hl: e.tensor_scalar(out=sels[:, hl, :], in0=gms[:, hl, :], scalar1=m8s[:, hl, 2:3],
                                                                     scalar2=None, op0=ALU.is_ge), reads=[B_gms, B_m8s], writes=[B_sels])
                    selv = sels[:].rearrange("p a b -> p (a b)")
                    S.op("dve", lambda e: e.tensor_scalar(out=selv, in0=selv, scalar1=NEGM, scalar2=-NEGM, op0=ALU.mult, op1=ALU.add),
                         writes=[B_sels])
                    ptr = ps_get()
                    for hl in range(4):
                        S.op("pe", lambda e, hl=hl: e.transpose(out=ps_t[ptr][0:32, hl * 8:(hl + 1) * 8], in_=sels[:, hl, :],
                                                                identity=identf[0:8, 0:8]),
                             reads=[B_sels, B_const], writes=[ps_b[ptr]])
                    S.op("act", lambda e: e.activation(out=negTs[:], in_=ps_t[ptr][0:32, 0:32], func=AF.Copy),
                         reads=[ps_b[ptr]], writes=[B_negTs])
                    ps_put(ptr)
                    po, pd = ps_get(), ps_get()
                    for qt in range(4):
                        psS = ps_get()
                        for k16 in range(16):
                            kt = qt * 16 + k16
                            mm_acc(psS, ps_t[psS][:, k16 * 32:(k16 + 1) * 32],
                                   [(K_[:, kt * 128:(kt + 1) * 128], qsg[:]), (E_sb[0:32, kt // 2, :], negTs[:])],
                                   [bK_, B_qsg, B_negTs, B_const])
                        p_, bp_ = pTs[pi_ % 2], B_pTs[pi_ % 2]
                        pi_ += 1
                        S.op("act", lambda e, p_=p_, psS=psS: e.activation(out=p_[:], in_=ps_t[psS][:, :512], func=AF.Exp, scale=SCALE),
                             reads=[ps_b[psS]], writes=[bp_])
                        ps_put(psS)
                        for k16 in range(16):
                            kt = qt * 16 + k16
                            first = (kt == 0)
                            S.op("pe", lambda e, p_=p_, kt=kt, k16=k16, first=first: e.matmul(
                                ps_t[po][:, :32], lhsT=V_[:, kt, :], rhs=p_[:, k16 * 32:(k16 + 1) * 32], start=first, stop=False),
                                reads=[bV_, bp_], writes=[ps_b[po]])
                            S.op("pe", lambda e, p_=p_, k16=k16, first=first: e.matmul(
                                ps_t[pd][:, :32], lhsT=ones[:], rhs=p_[:, k16 * 32:(k16 + 1) * 32], start=first, stop=False),
                                reads=[B_const, bp_], writes=[ps_b[pd]])
                    psn = ps_get()
                    mm_acc(psn, ps_t[psn][0:8, 0:32], [(ksamp[:, g, 8 * s:8 * s + 8], qsg[:]), (identb[0:8, 0:8], causs[:])],
                           [B_ksamp, B_qsg, B_const])
                    S.op("act", lambda e: e.activation(out=pTn[:], in_=ps_t[psn][0:8, 0:32], func=AF.Exp, scale=SCALE),
                         reads=[ps_b[psn]], writes=[B_pTn])
                    ps_put(psn)
                    S.op("pe", lambda e, s=s, g=g: e.matmul(ps_t[po][:, :32], lhsT=vsamp[0:8, s, g * 128:(g + 1) * 128], rhs=pTn[:],
                                                           start=False, stop=True), reads=[B_vsamp, B_pTn], writes=[ps_b[po]])
                    S.op("pe", lambda e: e.matmul(ps_t[pd][:, :32], lhsT=ones[0:8, :], rhs=pTn[:], start=False, stop=True),
                         reads=[B_const, B_pTn], writes=[ps_b[pd]])
                    S.op("dve", lambda e: e.reciprocal(out=rdens[:], in_=ps_t[pd][:, :32]), reads=[ps_b[pd]], writes=[B_rdens])
                    S.op("dve", lambda e, s=s, g=g: e.tensor_tensor(
                        out=attnT[:, 4 * g:4 * g + 4, NOWN + 8 * s:NOWN + 8 * s + 8],
                        in0=ps_t[po][:, :32].rearrange("p (a b) -> p a b", a=4), in1=rdens[:].rearrange("p (a b) -> p a b", a=4),
                        op=ALU.mult), reads=[ps_b[po], B_rdens], writes=[B_attn])
                    ps_put(po)
                    ps_put(pd)
            S.barrier()
            st3.close()
            stA.close()
            ck(7)
            tap("attnT", attnT, B_attn)

            stRn = contextlib.ExitStack()
            rnnT = sb("rnnT", [128, NCH, NT], BF16, stRn, top=True)
            B_rnn = Buf("rnn")
            st4 = contextlib.ExitStack()
            load_wrg(st4)
            N = NT
            xp = [sb(f"xpD{i}", [128, 3 + NOWN], F32, st4) for i in range(2)]
            xps = [sb(f"xpsD{i}", [128, 4, 11], F32, st4) for i in range(2)]
            B_xp = [Buf(), Buf()]
            xcf = sb("xcfD", [128, N], F32, st4)
            xcb = sb("xcbD", [128, N], BF16, st4)
            rr = sb("rrD", [128, N], F32, st4)
            ii = sb("iiD", [128, N], F32, st4)
            hh = sb("hhD", [128, N], F32, st4)
            gl = sb("glD", [128, N], F32, st4)
            B_xc, B_xcb, B_r, B_i, B_h, B_gl = Buf(), Buf(), Buf(), Buf(), Buf(), Buf()
            ohp = sb("ohp", [128, NCH], F32, st4)
            ocp = sb("ocp", [128, NCH, 3], F32, st4)
            ohs = sb("ohs", [128, NCH, 4], F32, st4)
            ocs = sb("ocs", [128, NCH, 4, 3], F32, st4)
            B_oh = Buf()
            for k in range(8):
                wx, bwx = w_next("XR")
                for gg in range(2):
                    n = 2 * k + gg
                    xq, xs_, bq = xp[n % 2], xps[n % 2], B_xp[n % 2]
                    S.op("pool", lambda e, xq=xq, n=n: e.tensor_copy(out=xq[:, 0:3], in_=halo[:, n, :]), reads=[B_halo], writes=[bq])
                    S.op("pool", lambda e, xs_=xs_, n=n: e.tensor_copy(out=xs_[:, :, 0:3], in_=stc[:, n, :, :]), reads=[B_const], writes=[bq])
                    for tt in range(3):
                        c0 = tt * TT
                        psi = ps_get()
                        mm_acc(psi, ps_t[psi][:, :TT],
                               [(wx[:, c, gg * 128:(gg + 1) * 128], uT[:, c, c0:c0 + TT]) for c in range(NCH)], [bwx, B_uT])
                        lo, hi = c0, min(c0 + TT, NOWN)
                        if hi > lo:
                            S.op("act", lambda e, xq=xq, lo=lo, hi=hi, c0=c0, psi=psi: e.activation(
                                out=xq[:, 3 + lo:3 + hi], in_=ps_t[psi][:, lo - c0:hi - c0], func=AF.Copy),
                                reads=[ps_b[psi]], writes=[bq])
                        if c0 + TT > NOWN:
                            S.op("act", lambda e, xs_=xs_, c0=c0, psi=psi: e.activation(
                                out=xs_[:, :, 3:11], in_=ps_t[psi][:, NOWN - c0:NT - c0].rearrange("p (s t) -> p s t", s=4), func=AF.Copy),
                                reads=[ps_b[psi]], writes=[bq])
                        ps_put(psi)
                    S.op("pool", lambda e, xq=xq, n=n: e.tensor_copy(out=ocp[:, n, :], in_=xq[:, NOWN:NOWN + 3]), reads=[bq], writes=[B_oh])
                    S.op("pool", lambda e, xs_=xs_, n=n: e.tensor_copy(out=ocs[:, n, :, :], in_=xs_[:, :, 8:11]), reads=[bq], writes=[B_oh])
                    rglru((xcf, xcb, rr, ii, hh), n, xq, NOWN, xs_, None, N, (bq, B_xc, B_xcb, B_r, B_i, B_h), None)
                    S.op("dve", lambda e, n=n: e.tensor_tensor(out=hcar[:, n:n + 1], in0=hcar[:, n:n + 1], in1=flg[:, 3:4], op=ALU.mult),
                         reads=[B_const], writes=[B_car])
                    S.op("dve", lambda e, n=n: e.tensor_tensor_scan(out=hh[:, :NOWN], data0=rr[:, :NOWN], data1=ii[:, :NOWN],
                                                                    initial=hcar[:, n:n + 1], op0=ALU.mult, op1=ALU.add),
                         reads=[B_r, B_i, B_car], writes=[B_h])
                    for s in range(4):
                        a0 = NOWN + 8 * s
                        S.op("dve", lambda e, n=n, s=s, a0=a0: e.tensor_tensor_scan(out=hh[:, a0:a0 + 8], data0=rr[:, a0:a0 + 8],
                                                                                    data1=ii[:, a0:a0 + 8], initial=sth[:, n, s:s + 1],
                                                                                    op0=ALU.mult, op1=ALU.add),
                             reads=[B_r, B_i, B_const], writes=[B_h])
                    S.op("pool", lambda e, n=n: e.tensor_copy(out=ohp[:, n:n + 1], in_=hh[:, NOWN - 1:NOWN]), reads=[B_h], writes=[B_oh])
                    S.op("pool", lambda e, n=n: e.tensor_copy(out=ohs[:, n, :], in_=hh[:, NOWN:NT].rearrange("p (s t) -> p s t", s=4)[:, :, 7]),
                         reads=[B_h], writes=[B_oh])
                    if gg == 0:
                        wy, bwy = w_next("YG")
                    for tt in range(3):
                        c0 = tt * TT
                        psi = ps_get()
                        mm_acc(psi, ps_t[psi][:, :TT],
                               [(wy[:, c, gg * 128:(gg + 1) * 128], uT[:, c, c0:c0 + TT]) for c in range(NCH)], [bwy, B_uT])
                        S.op("act", lambda e, c0=c0, psi=psi: e.activation(out=gl[:, c0:c0 + TT], in_=ps_t[psi][:, :TT], func=AF.Gelu_apprx_tanh),
                             reads=[ps_b[psi]], writes=[B_gl])
                        ps_put(psi)
                    S.op("dve", lambda e, n=n: e.tensor_tensor(out=rnnT[:, n, :], in0=hh[:, :N], in1=gl[:, :N], op=ALU.mult),
                         reads=[B_h, B_gl], writes=[B_rnn])
            S.dma("sp", d_out, o_hp, ohp[:], reads=[B_oh])
            S.dma("sp", d_out, o_cp, ocp[:], reads=[B_oh])
            S.dma("sp", d_out, o_hs, ohs[:], reads=[B_oh])
            S.dma("sp", d_out, o_cs, ocs[:], reads=[B_oh])
            S.barrier()
            st4.close()
            ck(8)
            tap("rnnT", rnnT, B_rnn)

            stMg = contextlib.ExitStack()
            mergedT = sb("mergedT", [128, NCH, NT], BF16, stMg, top=True)
            B_mrg = Buf("mrg")
            st5 = contextlib.ExitStack()
            sg_ = sb("sg_", [128, 2, NT], F32, st5)
            tm_ = sb("tm_", [128, 2, NT], F32, st5)
            t2_ = sb("t2_", [128, TT], F32, st5)
            B_sg, B_tm, B_t2 = Buf(), Buf(), Buf()
            for k in range(8):
                for half, (gtag, ptag, src, bsrc, pbias) in enumerate((("GA", "PA", rnnT, B_rnn, P_BGA), ("GB", "PB", attnT, B_attn, P_BGB))):
                    wg_, bwg = w_next(gtag)
                    for gg in range(2):
                        c_ = 2 * k + gg
                        for tt in range(3):
                            c0 = tt * TT
                            psi = ps_get()
                            mm_acc(psi, ps_t[psi][:, :TT],
                                   [(wg_[:, c, gg * 128:(gg + 1) * 128], uT[:, c, c0:c0 + TT]) for c in range(NCH)], [bwg, B_uT])
                            S.op("act", lambda e, gg=gg, c0=c0, psi=psi, c_=c_, pbias=pbias: e.activation(
                                out=sg_[:, gg, c0:c0 + TT], in_=ps_t[psi][:, :TT], func=AF.Sigmoid, bias=prm[:, pbias, c_:c_ + 1]),
                                reads=[ps_b[psi], B_const], writes=[B_sg])
                            ps_put(psi)
                    wp_, bwp = w_next(ptag)
                    for gg in range(2):
                        c_ = 2 * k + gg
                        for tt in range(3):
                            c0 = tt * TT
                            psi = ps_get()
                            mm_acc(psi, ps_t[psi][:, :TT],
                                   [(wp_[:, c, gg * 128:(gg + 1) * 128], src[:, c, c0:c0 + TT]) for c in range(NCH)], [bwp, bsrc])
                            if half == 0:
                                S.op("dve", lambda e, gg=gg, c0=c0, psi=psi: e.tensor_tensor(
                                    out=tm_[:, gg, c0:c0 + TT], in0=ps_t[psi][:, :TT], in1=sg_[:, gg, c0:c0 + TT], op=ALU.mult),
                                    reads=[ps_b[psi], B_sg], writes=[B_tm])
                            else:
                                S.op("dve", lambda e, gg=gg, c0=c0, psi=psi: e.tensor_tensor(
                                    out=t2_[:, :TT], in0=ps_t[psi][:, :TT], in1=sg_[:, gg, c0:c0 + TT], op=ALU.mult),
                                    reads=[ps_b[psi], B_sg], writes=[B_t2])
                                S.op("dve", lambda e, gg=gg, c0=c0, c_=c_: e.tensor_tensor(
                                    out=mergedT[:, c_, c0:c0 + TT], in0=t2_[:, :TT], in1=tm_[:, gg, c0:c0 + TT], op=ALU.add),
                                    reads=[B_t2, B_tm], writes=[B_mrg])
                            ps_put(psi)
            tap("mergedT", mergedT, B_mrg)
            S.barrier()
            st5.close()
            stAt.close()
            stRn.close()
            ck(9)

            u2T, B_u2 = uT, B_uT
            x1 = sb("x1", [128, NCH, NT], F32, stB, top=True)
            B_x1 = Buf("x1")
            st6 = contextlib.ExitStack()
            xre = [sb(f"xre{i}", [128, NT], F32, st6) for i in range(2)]
            B_xre = [Buf(), Buf()]
            d_xre = [S.dsem(f"xre{i}") for i in range(2)]
            for k in range(8):
                wt, bw = w_next("WO")
                for gg in range(2):
                    c_ = 2 * k + gg
                    xr_, bxr_ = xre[c_ % 2], B_xre[c_ % 2]
                    S.dma("sp", d_xre[c_ % 2], xr_[:], xT[c_ * 128:(c_ + 1) * 128, NPRE:NPRE + NT], writes=[bxr_])
                    for tt in range(3):
                        c0 = tt * TT
                        psi = ps_get()
                        mm_acc(psi, ps_t[psi][:, :TT],
                               [(wt[:, c, gg * 128:(gg + 1) * 128], mergedT[:, c, c0:c0 + TT]) for c in range(NCH)], [bw, B_mrg])
                        S.op("dve", lambda e, c_=c_, c0=c0, psi=psi, xr_=xr_: e.tensor_tensor(
                            out=x1[:, c_, c0:c0 + TT], in0=ps_t[psi][:, :TT], in1=xr_[:, c0:c0 + TT], op=ALU.add),
                            reads=[ps_b[psi], bxr_], writes=[B_x1])
                        ps_put(psi)
            tap("x1", x1, B_x1)
            S.barrier()
            st6.close()
            stMg.close()
            ck(10)

            def norm_x1(stk, gidx, out_fn, tag):
                def load(t0, w, t, b, d):
                    for c in range(NCH):
                        S.op("pool", lambda e, c=c: e.tensor_copy(out=t[:, c, :w], in_=x1[:, c, t0:t0 + w]), reads=[B_x1], writes=[b])
                rmsnorm(stk, load, NT, gidx, out_fn, tag=tag)

            st7 = contextlib.ExitStack()

            def out_u2(t0, w, xs, bx, rs, br):
                for c in range(NCH):
                    S.op("dve", lambda e, c=c: e.scalar_tensor_tensor(out=u2T[:, c, t0:t0 + w], in0=xs[:, c, :w],
                                                                      scalar=prm[:, P_NMLP, c:c + 1], in1=rs[:, :w],
                                                                      op0=ALU.mult, op1=ALU.mult),
                         reads=[bx, br, B_const], writes=[B_u2])
            norm_x1(st7, P_NMLP, out_u2, "m")
            S.barrier()
            st7.close()
            ck(11)

            st8 = contextlib.ExitStack()
            hff = sb("hff", [128, NCH, NT], BF16, st8)
            B_hff = Buf("hff")
            rl = [sb(f"rl{i}", [128, TT], F32, st8) for i in range(2)]
            B_rl = [Buf(), Buf()]
            ri = 0
            for q in range(4):
                for k in range(8):
                    wt, bw = w_next("F1")
                    for gg in range(2):
                        c_ = 2 * k + gg
                        for tt in range(3):
                            c0 = tt * TT
                            psi = ps_get()
                            mm_acc(psi, ps_t[psi][:, :TT],
                                   [(wt[:, c, gg * 128:(gg + 1) * 128], u2T[:, c, c0:c0 + TT]) for c in range(NCH)], [bw, B_u2])
                            r_, br_ = rl[ri % 2], B_rl[ri % 2]
                            ri += 1
                            S.op("act", lambda e, r_=r_, psi=psi: e.activation(out=r_[:, :TT], in_=ps_t[psi][:, :TT], func=AF.Relu),
                                 reads=[ps_b[psi]], writes=[br_])
                            ps_put(psi)
                            S.op("dve", lambda e, r_=r_, c_=c_, c0=c0: e.tensor_tensor(out=hff[:, c_, c0:c0 + TT], in0=r_[:, :TT], in1=r_[:, :TT],
                                                                                      op=ALU.mult), reads=[br_], writes=[B_hff])
                for k in range(8):
                    wt, bw = w_next("F2")
                    for gg in range(2):
                        c_ = 2 * k + gg
                        for tt in range(3):
                            c0 = tt * TT
                            psi = ps_get()
                            mm_acc(psi, ps_t[psi][:, :TT],
                                   [(wt[:, c, gg * 128:(gg + 1) * 128], hff[:, c, c0:c0 + TT]) for c in range(NCH)], [bw, B_hff])
                            S.op("dve", lambda e, c_=c_, c0=c0, psi=psi: e.tensor_tensor(
                                out=x1[:, c_, c0:c0 + TT], in0=ps_t[psi][:, :TT], in1=x1[:, c_, c0:c0 + TT], op=ALU.add),
                                reads=[ps_b[psi]], writes=[B_x1])
                            ps_put(psi)
            S.barrier()
            st8.close()
            stU.close()
            ck(12)

            st9 = contextlib.ExitStack()
            yo = [sb(f"yo{i}", [128, NCH, 256], F32, st9) for i in range(2)]
            B_yo = [Buf(), Buf()]
            yi = [0]

            def out_y(t0, w, xs, bx, rs, br):
                y_, by_ = yo[yi[0] % 2], B_yo[yi[0] % 2]
                dy_ = d_yo[yi[0] % 2]
                yi[0] += 1
                for c in range(NCH):
                    S.op("dve", lambda e, c=c: e.scalar_tensor_tensor(out=y_[:, c, :w], in0=xs[:, c, :w],
                                                                      scalar=prm[:, P_NFIN, c:c + 1], in1=rs[:, :w],
                                                                      op0=ALU.mult, op1=ALU.mult),
                         reads=[bx, br, B_const], writes=[by_])
                S.dma("sp", dy_, o_yT.rearrange("(c p) t -> p c t", p=128)[:, :, t0:t0 + w], y_[:, :, :w], reads=[by_])
            norm_x1(st9, P_NFIN, out_y, "f")
            S.barrier()
            st9.close()
            stB.close()
        except _Stop:
            pass
        S.waitfor("sp", reads=[])
        S.barrier()
    print("SBUF arena peak bytes/partition:", arena.peak, "of", arena.hi - arena.lo)
    return nc


def _rope_tables(pos):
    half = 64
    inv = (np.float32(10000.0) ** (-np.arange(half, dtype=np.float32) * np.float32(2.0 / 128))).astype(np.float32)
    ang = pos.astype(np.float32)[:, None] * inv[None, :]
    cos = np.cos(ang).astype(np.float32).T
    sin = np.sin(ang).astype(np.float32).T
    C = np.concatenate([cos, cos], axis=0)
    Ssig = np.concatenate([-sin, sin], axis=0)
    return np.ascontiguousarray(C), np.ascontiguousarray(Ssig)


def _consts():
    bf = ml_dtypes.bfloat16
    E = np.zeros((32, 32, 128), np.float32)
    for j in range(32):
        E[j, j, :] = 1.0
    ident = np.eye(128, dtype=np.float32)
    swap = np.zeros((128, 128), np.float32)
    for m in range(128):
        swap[(m + 64) % 128, m] = 1.0
    caus = np.zeros((128, 2, 256), np.float32)
    for kt2 in range(2):
        kk = kt2 * 128 + np.arange(128)[:, None]
        qq = np.arange(256)[None, :]
        caus[:, kt2, :] = np.where(kk <= qq, 0.0, -NEGM)
    causs = np.zeros((8, 32), np.float32)
    for t2 in range(8):
        for hl in range(4):
            for t in range(8):
                causs[t2, hl * 8 + t] = 0.0 if t2 <= t else -NEGM
    return {"c_E": E.astype(bf), "c_identb": ident.astype(bf), "c_identf": ident, "c_ones": np.ones((128, 128), bf),
            "c_swap": swap.astype(bf), "c_caus": caus.astype(bf), "c_causs": causs.astype(bf)}


def _fm(v):
    return np.ascontiguousarray(np.asarray(v, np.float32).reshape(NCH, 128).T)


_NC_CACHE = {}


def _weight_groups(ws):
    out = np.empty((NGRP, 128, NCH, GW), np.float32)
    g = 0
    for w in ws:
        w = np.asarray(w, np.float32)[0]
        rb, nb = w.shape[0] // D, w.shape[1] // GW
        v = w.reshape(rb, NCH, 128, nb, GW).transpose(0, 3, 2, 1, 4)
        out[g:g + rb * nb] = v.reshape(rb * nb, 128, NCH, GW)
        g += rb * nb
    assert g == NGRP
    return out.reshape(NGRP * 128, NCH * GW)


def _cache_layouts(cache_k, cache_v):
    f32 = np.float32
    ck = np.ascontiguousarray(np.asarray(cache_k, f32)[0].transpose(0, 3, 2, 1)).reshape(-1, 256)
    cv = np.ascontiguousarray(np.asarray(cache_v, f32)[0]).reshape(-1, 256)
    return ck, cv


def make_in_maps(x_prompt, x_sample, cache_k, cache_v, state_h, state_conv, page_table,
                 norm_mix, w_in, b_gate, conv_w, conv_b, w_rg_a, b_rg_a, w_rg_x, b_rg_x,
                 lru_lambda, w_proj_a, w_proj_b, w_out, norm_mlp, w_ff1, w_ff2, norm_final, cores=range(8), pool_fn=None):
    f32 = np.float32
    x_prompt = np.asarray(x_prompt, f32)
    x_sample = np.asarray(x_sample, f32)
    consts = _consts()
    pp = np.zeros((128, NPRM, NCH), f32)
    pp[:, P_NMIX] = _fm(norm_mix[0]); pp[:, P_NMLP] = _fm(norm_mlp[0]); pp[:, P_NFIN] = _fm(norm_final)
    for j in range(4):
        pp[:, P_CW0 + j] = _fm(conv_w[0, j])
    pp[:, P_CB] = _fm(conv_b[0]); pp[:, P_BRA] = _fm(b_rg_a[0]); pp[:, P_BRX] = _fm(b_rg_x[0])
    pp[:, P_LAM] = _fm(lru_lambda[0]); pp[:, P_BGA] = _fm(b_gate[0, :D]); pp[:, P_BGB] = _fm(b_gate[0, D:])
    shared = {
        "wall": _weight_groups([w_in, w_proj_a, w_proj_b, w_out, w_ff1, w_ff2]),
        "w_rga": np.ascontiguousarray(np.asarray(w_rg_a, f32)[0].transpose(1, 0, 2)).reshape(128, 2048),
        "w_rgx": np.ascontiguousarray(np.asarray(w_rg_x, f32)[0].transpose(1, 0, 2)).reshape(128, 2048),
        "pp": pp,
    }
    if pool_fn is None:
        shared["ckT"], shared["cv"] = _cache_layouts(cache_k, cache_v)
    shared.update(consts)
    xpT = [np.ascontiguousarray(x_prompt[b].T) for b in range(2)]
    in_maps = []
    for c in cores:
        b, r = c // 4, c % 4
        pad = (3 - r) * 1024
        xT = np.zeros((D, NCOL), f32)
        xT[:, pad:NPRE + NOWN] = xpT[b][:, :1024 * (r + 1)]
        xs = x_sample[4 * c:4 * c + 4].reshape(32, D)
        xT[:, NPRE + NOWN:] = xs.T
        pos = np.concatenate([np.maximum(np.arange(NPRE + NOWN) - pad, 0), 8192 + np.tile(np.arange(8), 4)]).astype(np.int64)
        C, Ssig = _rope_tables(pos)
        flags = np.ones((128, 4), f32)
        flags[:, 3 - r] = 0.0
        padb = pad // 256
        pastadd = np.zeros((128, 8, 16), f32)
        pastval = np.zeros((128, 8, 16), f32)
        ownind = np.zeros((128, 8, 16), f32)
        for i in range(8):
            qb_ = 12 + i // 2
            for kbk in range(16):
                ok = (kbk >= padb) and (kbk < qb_)
                pastval[:, i, kbk] = 1.0 if ok else 0.0
                pastadd[:, i, kbk] = 0.0 if ok else -1e30
            ownind[:, i, qb_] = 1.0
        m = dict(shared)
        ptc = np.ascontiguousarray(np.asarray(page_table, np.int32)[4 * c:4 * c + 4].reshape(1, 256))
        if pool_fn is not None:
            ptc, m["ckT"], m["cv"] = pool_fn(ptc)
        m.update({
            "xT": xT, "ropeC": C, "ropeS": Ssig, "flags": flags,
            "st_h": np.ascontiguousarray(np.asarray(state_h, f32)[0, 4 * c:4 * c + 4].reshape(4, NCH, 128).transpose(2, 1, 0)),
            "st_c": np.ascontiguousarray(np.asarray(state_conv, f32)[0, 4 * c:4 * c + 4].reshape(4, 3, NCH, 128).transpose(3, 2, 0, 1)),
            "pt": ptc,
            "pastadd": pastadd, "pastval": pastval, "ownind": ownind,
        })
        in_maps.append(m)
    return in_maps


def kernel(x_prompt, x_sample, cache_k, cache_v, state_h, state_conv, page_table,
           norm_mix, w_in, b_gate, conv_w, conv_b, w_rg_a, b_rg_a, w_rg_x, b_rg_x,
           lru_lambda, w_proj_a, w_proj_b, w_out, norm_mlp, w_ff1, w_ff2, norm_final):
    f32 = np.float32
    in_maps = make_in_maps(x_prompt, x_sample, cache_k, cache_v, state_h, state_conv, page_table,
                           norm_mix, w_in, b_gate, conv_w, conv_b, w_rg_a, b_rg_a, w_rg_x, b_rg_x,
                           lru_lambda, w_proj_a, w_proj_b, w_out, norm_mlp, w_ff1, w_ff2, norm_final)
    if "nc" not in _NC_CACHE:
        _NC_CACHE["nc"] = build_nc()
    nc = _NC_CACHE["nc"]
    res = run_bass_kernel_spmd(nc, in_maps, core_ids=list(range(8)))
    R = res.results
    y_prompt = np.zeros((2, SEQ, D), f32)
    y_sample = np.zeros((32, 8, D), f32)
    k_prompt = np.zeros((1, 2, SEQ, 4, 128), f32)
    v_prompt = np.zeros((1, 2, SEQ, 4, 128), f32)
    h_prompt = np.zeros((1, 2, D), f32)
    conv_prompt = np.zeros((1, 2, 3, D), f32)
    k_sample = np.zeros((1, 32, 8, 4, 128), f32)
    v_sample = np.zeros((1, 32, 8, 4, 128), f32)
    h_sample = np.zeros((1, 32, D), f32)
    conv_sample = np.zeros((1, 32, 3, D), f32)
    for c in range(8):
        b, r = c // 4, c % 4
        o = R[c]
        yT = np.asarray(o["o_yT"])
        y_prompt[b, 1024 * r:1024 * (r + 1)] = yT[:, :NOWN].T
        y_sample[4 * c:4 * c + 4] = yT[:, NOWN:].T.reshape(4, 8, D)
        kT = np.asarray(o["o_kT"])
        k_prompt[0, b, 1024 * r:1024 * (r + 1)] = kT[:, :NOWN].T.reshape(NOWN, 4, 128)
        k_sample[0, 4 * c:4 * c + 4] = kT[:, NOWN:].T.reshape(4, 8, 4, 128)
        v = np.asarray(o["o_v"])
        v_prompt[0, b, 1024 * r:1024 * (r + 1)] = v[:NOWN].reshape(NOWN, 4, 128)
        v_sample[0, 4 * c:4 * c + 4] = v[NOWN:].reshape(4, 8, 4, 128)
        if r == 3:
            h_prompt[0, b] = np.asarray(o["o_hp"]).T.reshape(D)
            conv_prompt[0, b] = np.asarray(o["o_cp"]).transpose(2, 1, 0).reshape(3, D)
        h_sample[0, 4 * c:4 * c + 4] = np.asarray(o["o_hs"]).transpose(2, 1, 0).reshape(4, D)
        conv_sample[0, 4 * c:4 * c + 4] = np.asarray(o["o_cs"]).transpose(2, 3, 1, 0).reshape(4, 3, D)
    if DEBUG:
        kernel.debug = R
    return (y_prompt, y_sample, k_prompt, v_prompt, h_prompt, conv_prompt, k_sample, v_sample, h_sample, conv_sample)
```
